# Optimizing a Trainium2 kernel written in Bass

```python
import math
import jax, jax.numpy as jnp
from jax import lax
import numpy as np

D_MODEL = 1024
BATCH = 4
SEQ = 8192
DEPTH = 4

N_MIXERS = 3
GRID_W = 64
Q_BLOCK = 128
EPS = 1e-6
D_FF = 2816
FFN_RES = 0.5

DIFF_HEAD_DIM = 64
DIFF_HEADS = D_MODEL // (2 * DIFF_HEAD_DIM)
ROPE_THETA = 500000.0
ROPE_DIMS = DIFF_HEAD_DIM // 4

NA_HEAD_DIM = 64
NA_HEADS = D_MODEL // NA_HEAD_DIM
NA_WIN_H = 8
NA_WIN_W = 16

GQA_HEAD_DIM = 64
GQA_Q_HEADS = D_MODEL // GQA_HEAD_DIM
GQA_KV_HEADS = GQA_Q_HEADS // 4
AXIAL_THETA = 10000.0

N_A = len(range(0, DEPTH, N_MIXERS))
N_B = len(range(1, DEPTH, N_MIXERS))
N_C = len(range(2, DEPTH, N_MIXERS))

kernel_name = 'hybrid_diff_na_gqa_macaron_encoder'


def _rmsnorm(x, g):
    xf = x.astype(jnp.float32)
    y = xf * lax.rsqrt(jnp.mean(xf * xf, axis=-1, keepdims=True) + EPS)
    return (y * g.astype(jnp.float32)).astype(x.dtype)


def _swiglu(x, wg, wu, wd):
    return (jax.nn.silu(x @ wg) * (x @ wu)) @ wd


def _rope_angles(pos, dims, theta):
    inv = theta ** (-jnp.arange(0, dims, 2, dtype=jnp.float32) / dims)
    return pos[:, None] * inv[None, :]


def _rotate(x, ang):
    m = ang.shape[1]
    ang = ang.reshape((1, ang.shape[0]) + (1,) * (x.ndim - 3) + (m,))
    cos = jnp.cos(ang).astype(x.dtype)
    sin = jnp.sin(ang).astype(x.dtype)
    x1, x2 = x[..., :m], x[..., m:]
    return jnp.concatenate([x1 * cos - x2 * sin, x2 * cos + x1 * sin], axis=-1)


def _to_blocks(q):
    b, s = q.shape[:2]
    return jnp.moveaxis(q.reshape((b, s // Q_BLOCK, Q_BLOCK) + q.shape[2:]), 1, 0)


def _from_blocks(o):
    o = jnp.moveaxis(o, 0, 1)
    return o.reshape((o.shape[0], o.shape[1] * o.shape[2]) + o.shape[3:])


def _diff_attention(h, w_in, w_out, lam_vecs, subln_g, layer_idx):
    b, s, _ = h.shape
    nh, dh = DIFF_HEADS, DIFF_HEAD_DIM
    q, k, v = jnp.split(h @ w_in, 3, axis=-1)
    q = q.reshape(b, s, nh, 2, dh)
    k = k.reshape(b, s, nh, 2, dh)
    v = v.reshape(b, s, nh, 2 * dh)
    ang = _rope_angles(jnp.arange(s, dtype=jnp.float32), ROPE_DIMS, ROPE_THETA)
    q = jnp.concatenate([_rotate(q[..., :ROPE_DIMS], ang), q[..., ROPE_DIMS:]], axis=-1)
    k = jnp.concatenate([_rotate(k[..., :ROPE_DIMS], ang), k[..., ROPE_DIMS:]], axis=-1)
    lam_init = 0.8 - 0.6 * math.exp(-0.3 * layer_idx)
    lv = lam_vecs.astype(jnp.float32)
    lam = jnp.exp(jnp.sum(lv[0] * lv[1])) - jnp.exp(jnp.sum(lv[2] * lv[3])) + lam_init
    scale = dh ** -0.5

    def block(qb):
        sc = jnp.einsum('bqhcd,bshcd->bhcqs', qb, k).astype(jnp.float32) * scale
        p = jax.nn.softmax(sc, axis=-1)
        a = (p[:, :, 0] - lam * p[:, :, 1]).astype(v.dtype)
        return jnp.einsum('bhqs,bshe->bqhe', a, v)

    o = _from_blocks(lax.map(block, _to_blocks(q)))
    o = _rmsnorm(o, subln_g) * (1.0 - lam_init)
    return o.reshape(b, s, nh * 2 * dh) @ w_out


def _neighborhood_attention(h, w_in, w_out, rpb):
    b, s, _ = h.shape
    nh, dh = NA_HEADS, NA_HEAD_DIM
    rows = s // GRID_W
    kh = min(NA_WIN_H, rows)
    kw = NA_WIN_W
    q, k, v = jnp.split(h @ w_in, 3, axis=-1)
    q = q.reshape(b, rows, GRID_W, nh, dh)
    k = k.reshape(b, rows, GRID_W, nh, dh)
    v = v.reshape(b, rows, GRID_W, nh, dh)
    cols = jnp.arange(GRID_W)
    col_start = jnp.clip(cols - kw // 2, 0, GRID_W - kw)
    col_idx = col_start[:, None] + jnp.arange(kw)[None, :]
    dc = col_idx - cols[:, None] + (NA_WIN_W - 1)
    scale = dh ** -0.5

    def row_step(r):
        r0 = jnp.clip(r - kh // 2, 0, rows - kh)
        dr = r0 + jnp.arange(kh) - r + (NA_WIN_H - 1)
        k_band = lax.dynamic_slice_in_dim(k, r0, kh, axis=1)
        v_band = lax.dynamic_slice_in_dim(v, r0, kh, axis=1)
        k_win = k_band[:, :, col_idx]
        v_win = v_band[:, :, col_idx]
        q_row = lax.dynamic_index_in_dim(q, r, axis=1, keepdims=False)
        bias = rpb[:, dr[None, :, None], dc[:, None, :]]
        sc = jnp.einsum('bqhd,brqjhd->bhqrj', q_row, k_win).astype(jnp.float32) * scale
        sc = sc + bias.astype(jnp.float32)[None]
        p = jax.nn.softmax(sc.reshape(b, nh, GRID_W, kh * kw), axis=-1)
        p = p.reshape(sc.shape).astype(v.dtype)
        return jnp.einsum('bhqrj,brqjhd->bqhd', p, v_win)

    o = lax.map(row_step, jnp.arange(rows))
    o = jnp.moveaxis(o, 0, 1).reshape(b, s, nh * dh)
    return o @ w_out


def _gqa_axial(h, w_in, w_out, qk_g):
    b, s, _ = h.shape
    nq, nkv, dh = GQA_Q_HEADS, GQA_KV_HEADS, GQA_HEAD_DIM
    grp = nq // nkv
    qkv = h @ w_in
    q = qkv[..., :nq * dh].reshape(b, s, nkv, grp, dh)
    k = qkv[..., nq * dh:(nq + nkv) * dh].reshape(b, s, nkv, dh)
    v = qkv[..., (nq + nkv) * dh:].reshape(b, s, nkv, dh)
    q = _rmsnorm(q, qk_g[0])
    k = _rmsnorm(k, qk_g[1])
    t = jnp.arange(s)
    half = dh // 2
    ang_r = _rope_angles((t // GRID_W).astype(jnp.float32), half, AXIAL_THETA)
    ang_c = _rope_angles((t % GRID_W).astype(jnp.float32), half, AXIAL_THETA)
    q = jnp.concatenate([_rotate(q[..., :half], ang_r), _rotate(q[..., half:], ang_c)], axis=-1)
    k = jnp.concatenate([_rotate(k[..., :half], ang_r), _rotate(k[..., half:], ang_c)], axis=-1)
    scale = dh ** -0.5

    def block(qb):
        sc = jnp.einsum('bqkgd,bskd->bkgqs', qb, k).astype(jnp.float32) * scale
        p = jax.nn.softmax(sc, axis=-1).astype(v.dtype)
        return jnp.einsum('bkgqs,bskd->bqkgd', p, v)

    o = _from_blocks(lax.map(block, _to_blocks(q)))
    return o.reshape(b, s, nq * dh) @ w_out


def setup_inputs(seed: int = 0) -> dict:
    key = jax.random.key(seed)
    ks = jax.random.split(key, 20)

    def w(k, shape, fan_in):
        return jax.random.normal(k, shape, jnp.float32) * fan_in ** -0.5

    def gain(k, shape):
        return 1.0 + 0.05 * jax.random.normal(k, shape, jnp.float32)

    diff_w = DIFF_HEADS * 2 * DIFF_HEAD_DIM
    na_w = NA_HEADS * NA_HEAD_DIM
    gqa_q = GQA_Q_HEADS * GQA_HEAD_DIM
    gqa_in = (GQA_Q_HEADS + 2 * GQA_KV_HEADS) * GQA_HEAD_DIM
    return {
        'x': jax.random.normal(ks[0], (BATCH, SEQ, D_MODEL), jnp.float32),
        'norm_g': gain(ks[1], (DEPTH, 6, D_MODEL)),
        'ffn1_wg': w(ks[2], (DEPTH, D_MODEL, D_FF), D_MODEL),
        'ffn1_wu': w(ks[3], (DEPTH, D_MODEL, D_FF), D_MODEL),
        'ffn1_wd': w(ks[4], (DEPTH, D_FF, D_MODEL), D_FF),
        'ffn2_wg': w(ks[5], (DEPTH, D_MODEL, D_FF), D_MODEL),
        'ffn2_wu': w(ks[6], (DEPTH, D_MODEL, D_FF), D_MODEL),
        'ffn2_wd': w(ks[7], (DEPTH, D_FF, D_MODEL), D_FF),
        'diff_w_in': w(ks[8], (N_A, D_MODEL, 3 * diff_w), D_MODEL),
        'diff_w_out': w(ks[9], (N_A, diff_w, D_MODEL), diff_w),
        'diff_lambda': 0.1 * jax.random.normal(ks[10], (N_A, 4, DIFF_HEAD_DIM), jnp.float32),
        'diff_subln': gain(ks[11], (N_A, 2 * DIFF_HEAD_DIM)),
        'na_w_in': w(ks[12], (N_B, D_MODEL, 3 * na_w), D_MODEL),
        'na_w_out': w(ks[13], (N_B, na_w, D_MODEL), na_w),
        'na_rpb': 0.1 * jax.random.normal(ks[14], (N_B, NA_HEADS, 2 * NA_WIN_H - 1, 2 * NA_WIN_W - 1), jnp.float32),
        'gqa_w_in': w(ks[15], (N_C, D_MODEL, gqa_in), D_MODEL),
        'gqa_w_out': w(ks[16], (N_C, gqa_q, D_MODEL), gqa_q),
        'gqa_qk_norm': gain(ks[17], (N_C, 2, GQA_HEAD_DIM)),
    }


def reference(x, norm_g, ffn1_wg, ffn1_wu, ffn1_wd, ffn2_wg, ffn2_wu, ffn2_wd,
              diff_w_in, diff_w_out, diff_lambda, diff_subln,
              na_w_in, na_w_out, na_rpb,
              gqa_w_in, gqa_w_out, gqa_qk_norm):
    h = x
    for i in range(DEPTH):
        g = norm_g[i]
        h = h + FFN_RES * _rmsnorm(_swiglu(_rmsnorm(h, g[0]), ffn1_wg[i], ffn1_wu[i], ffn1_wd[i]), g[1])
        u = _rmsnorm(h, g[2])
        kind, j = i % N_MIXERS, i // N_MIXERS
        if kind == 0:
            m = _diff_attention(u, diff_w_in[j], diff_w_out[j], diff_lambda[j], diff_subln[j], i)
        elif kind == 1:
            m = _neighborhood_attention(u, na_w_in[j], na_w_out[j], na_rpb[j])
        else:
            m = _gqa_axial(u, gqa_w_in[j], gqa_w_out[j], gqa_qk_norm[j])
        h = h + _rmsnorm(m, g[3])
        h = h + FFN_RES * _rmsnorm(_swiglu(_rmsnorm(h, g[4]), ffn2_wg[i], ffn2_wu[i], ffn2_wd[i]), g[5])
    return h
```

```python
import numpy as np
import ml_dtypes
from contextlib import ExitStack
import concourse.bass as bass
import concourse.mybir as mybir
from concourse.bass_utils import run_bass_kernel_spmd

F32 = mybir.dt.float32
BF16 = mybir.dt.bfloat16
ALU = mybir.AluOpType
AF = mybir.ActivationFunctionType

D = 1024
DFF = 2816
import os
TOK = int(os.environ.get("KTOK", "4096"))
SEQ = 2 * TOK
BATCH = 4
NCORES = 8
TT = 256
EPS = 1e-6
NKC = D // 128
SAFE_SYNC = bool(os.environ.get("KSAFE"))
NFC = DFF // 128


class Tile:
    __slots__ = ("ap", "w", "r")

    def __init__(self, ap):
        self.ap = ap
        self.w = None
        self.r = {}


class KB:
    def __init__(self, nc, es):
        self.nc = nc
        self.es = es
        self.eng = {"pe": nc.tensor, "act": nc.scalar, "dve": nc.vector, "pool": nc.gpsimd, "sp": nc.sync}
        self.sems = {}
        self.cnt = {}
        for e in ("pe", "act", "dve", "pool"):
            self.sems[e] = es.enter_context(nc.semaphore("s_" + e))
            self.cnt[e] = 0
        self.nslots = 6
        self.dslot = {}
        for q in ("sp", "pool", "act"):
            self.dslot[q] = 0
            for s in range(self.nslots):
                k = ("d", q, s)
                self.sems[k] = es.enter_context(nc.semaphore("s_d%s%d" % (q, s)))
                self.cnt[k] = 0
        self.sems["cc"] = es.enter_context(nc.semaphore("s_cc"))
        self.cnt["cc"] = 0
        self.known = {e: {} for e in self.eng}
        self.pend_r = {e: [] for e in self.eng}
        self.pend_w = {e: [] for e in self.eng}
        self.uid = 0

    def sb(self, es, shape, dt, name=None):
        self.uid += 1
        return es.enter_context(self.nc.sbuf_tensor("%s_%d" % (name or "t", self.uid), list(shape), dt))

    def ps(self, es, shape, dt=F32, name=None):
        self.uid += 1
        return es.enter_context(self.nc.psum_tensor("%s_%d" % (name or "p", self.uid), list(shape), dt))

    def _wait(self, e, tok):
        k, v = tok
        if k == e and e == "pe":
            return
        if self.known[e].get(k, 0) >= v:
            return
        self.eng[e].wait_ge(self.sems[k], v)
        self.known[e][k] = v

    def _deps(self, reads, writes, e=None):
        deps = {}
        for t in reads:
            if t.w is not None:
                k, v = t.w
                deps[k] = max(deps.get(k, 0), v)
        for t in writes:
            if t.w is not None:
                k, v = t.w
                if k != e or SAFE_SYNC or self.cnt[e] - v < 2:
                    deps[k] = max(deps.get(k, 0), v)
            for k, v in t.r.items():
                if k != e or SAFE_SYNC or self.cnt[e] - v < 2:
                    deps[k] = max(deps.get(k, 0), v)
        return deps

    def _record(self, e, tok, reads, writes):
        k, v = tok
        for t in self.pend_r[e]:
            t.r[k] = v
        for t in self.pend_w[e]:
            t.w = tok
            t.r = {}
        self.pend_r[e] = []
        self.pend_w[e] = []
        for t in reads:
            t.r[k] = v
        for t in writes:
            t.w = tok
            t.r = {}

    def op(self, e, fn, reads=(), writes=(), signal=True):
        for k, v in self._deps(reads, writes, e).items():
            self._wait(e, (k, v))
        ins = fn(self.eng[e])
        if signal:
            self.cnt[e] += 1
            ins.then_inc(self.sems[e], 1)
            self._record(e, (e, self.cnt[e]), reads, writes)
        else:
            self.pend_r[e].extend(reads)
            self.pend_w[e].extend(writes)
        return ins

    def dma(self, q, out, in_, reads=(), writes=()):
        s = self.dslot[q]
        self.dslot[q] = (s + 1) % self.nslots
        k = ("d", q, s)
        deps = self._deps(reads, writes)
        if self.cnt[k] > 0:
            deps[k] = max(deps.get(k, 0), self.cnt[k])
        for kk, v in deps.items():
            self._wait(q, (kk, v))
        ins = self.eng[q].dma_start(out=out, in_=in_)
        self.cnt[k] += 16
        ins.then_inc(self.sems[k], 16)
        tok = (k, self.cnt[k])
        for t in reads:
            t.r[k] = self.cnt[k]
        for t in writes:
            t.w = tok
            t.r = {}
        return ins

    def allgather_pairs(self, src, dst):
        self.cnt["cc"] += 1
        self.nc.gpsimd.collective_compute("AllGather", ALU.bypass, replica_groups=[[0, 1], [2, 3], [4, 5], [6, 7]],
                                          ins=[src.opt()], outs=[dst.opt()]).then_inc(self.sems["cc"])

    def barrier(self):
        for e in self.eng:
            for k, v in self.cnt.items():
                if v > 0:
                    self._wait(e, (k, v))


class Ctx:
    pass


def setup_consts(kb, es, ng_dram):
    nc = kb.nc
    c = Ctx()
    c.ones_d = Tile(kb.sb(es, [128, 128], BF16, "ones_d")[:])
    c.ones_1 = Tile(kb.sb(es, [128, 128], BF16, "ones_1")[:])
    c.ones_128 = Tile(kb.sb(es, [128, 128], BF16, "ones_128")[:])
    c.blk64 = Tile(kb.sb(es, [128, 128], BF16, "blk64")[:])
    kb.op("dve", lambda e: e.memset(c.ones_d.ap, 1.0 / 1024.0), writes=[c.ones_d])
    kb.op("dve", lambda e: e.memset(c.ones_1.ap, 1.0), writes=[c.ones_1])
    kb.op("dve", lambda e: e.memset(c.ones_128.ap, 1.0 / 128.0), writes=[c.ones_128])
    kb.op("dve", lambda e: e.memset(c.blk64.ap, 0.0), writes=[c.blk64])
    kb.op("dve", lambda e: e.memset(c.blk64.ap[0:64, 0:64], 1.0 / 64.0), writes=[c.blk64])
    kb.op("dve", lambda e: e.memset(c.blk64.ap[64:128, 64:128], 1.0 / 64.0), writes=[c.blk64])
    c.ng = Tile(kb.sb(es, [128, 4, 6, NKC], F32, "ng")[:])
    kb.dma("sp", out=c.ng.ap, in_=ng_dram, writes=[c.ng])
    c.ngh = Tile(kb.sb(es, [128, 4, 6, NKC], F32, "ngh")[:])
    kb.op("dve", lambda e: e.tensor_scalar_mul(out=c.ngh.ap, in0=c.ng.ap, scalar1=0.5), reads=[c.ng], writes=[c.ngh])
    return c


def hview(hT, t, tt=TT):
    return hT.rearrange("(c p) t -> p c t", p=128)[:, :, t * tt:(t + 1) * tt]


def rstd_from_ps(kb, ps_ss, rstd, eps=EPS, tt=TT):
    kb.op("act", lambda e: e.activation(out=rstd.ap, in_=ps_ss.ap[:, 0:tt], func=AF.Sqrt, bias=eps, scale=1.0),
          reads=[ps_ss], writes=[rstd])
    kb.op("dve", lambda e: e.reciprocal(out=rstd.ap, in_=rstd.ap), reads=[rstd], writes=[rstd])


def prenorm(kb, c, h, sq, ps_ss, rstd, u_t, g_ap_fn):
    kb.op("act", lambda e: e.activation(out=sq.ap, in_=h.ap, func=AF.Square), reads=[h], writes=[sq])
    for k in range(NKC):
        kb.op("pe", lambda e, k=k: e.matmul(ps_ss.ap[:, 0:TT], lhsT=c.ones_d.ap, rhs=sq.ap[:, k, :], start=(k == 0), stop=(k == NKC - 1)),
              reads=[sq, c.ones_d], writes=[ps_ss], signal=(k == NKC - 1))
    rstd_from_ps(kb, ps_ss, rstd)
    for k in range(NKC):
        kb.op("dve", lambda e, k=k: e.scalar_tensor_tensor(out=u_t[k].ap, in0=h.ap[:, k, :], scalar=g_ap_fn(k), in1=rstd.ap,
                                                         op0=ALU.mult, op1=ALU.mult),
              reads=[h, rstd, c.ng, c.ngh], writes=[u_t[k]])


def postnorm_residual(kb, c, psy, h, sq_t, ps_ss, rstd, tmp_t, g_ap_fn, tt=TT):
    for i in range(NKC):
        kb.op("act", lambda e, i=i: e.activation(out=sq_t[i].ap, in_=psy[i][1], func=AF.Square),
              reads=[psy[i][0]], writes=[sq_t[i]])
    for i in range(NKC):
        kb.op("pe", lambda e, i=i: e.matmul(ps_ss.ap[:, 0:tt], lhsT=c.ones_d.ap, rhs=sq_t[i].ap, start=(i == 0), stop=(i == NKC - 1)),
              reads=[sq_t[i], c.ones_d], writes=[ps_ss], signal=(i == NKC - 1))
    rstd_from_ps(kb, ps_ss, rstd, tt=tt)
    for i in range(NKC):
        tmp = tmp_t[i % len(tmp_t)]
        kb.op("dve", lambda e, i=i, tmp=tmp: e.scalar_tensor_tensor(out=tmp.ap, in0=psy[i][1], scalar=g_ap_fn(i), in1=rstd.ap,
                                                                   op0=ALU.mult, op1=ALU.mult),
              reads=[psy[i][0], rstd, c.ng, c.ngh], writes=[tmp])
        kb.op("pool", lambda e, i=i, tmp=tmp: e.tensor_tensor(out=h.ap[:, i, :], in0=h.ap[:, i, :], in1=tmp.ap, op=ALU.add),
              reads=[tmp, h], writes=[h])


def load_cast_weight(kb, stg, src2d, dst_tiles, dst_ap_fn, nrow_chunks, ncol, state):
    half = 1408
    for k in range(nrow_chunks):
        for c0 in range(0, ncol, half):
            cw = min(half, ncol - c0)
            i = state[0]
            state[0] += 1
            st = stg[i % len(stg)]
            kb.dma("sp" if i % 2 == 0 else "pool", out=st.ap[:, :cw], in_=src2d[k * 128:(k + 1) * 128, c0:c0 + cw], writes=[st])
            ce = ("dve", "act")[i % 2]
            if ce == "act":
                kb.op("act", lambda e, st=st, k=k, c0=c0, cw=cw: e.copy(out=dst_ap_fn(k)[:, c0:c0 + cw], in_=st.ap[:, :cw]),
                      reads=[st], writes=[])
            else:
                kb.op("dve", lambda e, st=st, k=k, c0=c0, cw=cw: e.tensor_copy(out=dst_ap_fn(k)[:, c0:c0 + cw], in_=st.ap[:, :cw]),
                      reads=[st], writes=[])
            dst_tiles[k].append((ce, kb.cnt[ce]))


class WChunk:
    def __init__(self):
        self.toks = []


def wait_toks(kb, e, toks):
    for t in toks:
        kb._wait(e, t)


def ffn_phase(kb, c, hT_in, hT_out, wg, wu, wd, layer, jpre, jpost, ntiles=TOK // TT):
    nc = kb.nc
    with ExitStack() as es:
        wgb = kb.sb(es, [128, NKC, DFF], BF16, "wgb")
        wub = kb.sb(es, [128, NKC, DFF], BF16, "wub")
        wdb = kb.sb(es, [128, NFC, D], BF16, "wdb")
        stg = [Tile(kb.sb(es, [128, 1408], F32, "stg")[:]) for _ in range(3)]
        hb = [Tile(kb.sb(es, [128, NKC, TT], F32, "hb")[:]) for _ in range(2)]
        sq = Tile(kb.sb(es, [128, NKC, TT], BF16, "sq")[:])
        sq2_raw = kb.sb(es, [128, NKC, TT], BF16, "sq2")
        sq2_t = [Tile(sq2_raw[:, i, :]) for i in range(NKC)]
        u_raw = kb.sb(es, [128, NKC, TT], BF16, "u")
        u_t = [Tile(u_raw[:, k, :]) for k in range(NKC)]
        sg = [Tile(kb.sb(es, [128, TT], F32, "sg")[:]) for _ in range(2)]
        act = [Tile(kb.sb(es, [128, TT], BF16, "a")[:]) for _ in range(3)]
        tmp_t = [Tile(kb.sb(es, [128, TT], F32, "tmp")[:]) for _ in range(2)]
        rstd = Tile(kb.sb(es, [128, TT], F32, "rstd")[:])
        rstd2 = Tile(kb.sb(es, [128, TT], F32, "rstd2")[:])
        pgu = [Tile(kb.ps(es, [128, 512], F32, "pgu")[:]) for _ in range(2)]
        psyb = [Tile(kb.ps(es, [128, 512], F32, "py")[:]) for _ in range(4)]
        ps_ss = Tile(kb.ps(es, [128, 512], F32, "pss")[:])
        ps_ss2 = Tile(kb.ps(es, [128, 512], F32, "pss2")[:])
        psy = [(psyb[i // 2], psyb[i // 2].ap[:, (i % 2) * TT:(i % 2 + 1) * TT]) for i in range(NKC)]

        wg_tok = [[] for _ in range(NKC)]
        wu_tok = [[] for _ in range(NKC)]
        wd_tok = [[] for _ in range(NFC)]
        state = [0]
        load_cast_weight(kb, stg, wg, wg_tok, lambda k: wgb[:, k, :], NKC, DFF, state)
        load_cast_weight(kb, stg, wu, wu_tok, lambda k: wub[:, k, :], NKC, DFF, state)
        load_cast_weight(kb, stg, wd, wd_tok, lambda k: wdb[:, k, :], NFC, D, state)
        for toks in wg_tok + wu_tok + wd_tok:
            wait_toks(kb, "pe", toks)

        gpre = lambda k: c.ng.ap[:, layer, jpre, k:k + 1]
        gpost = lambda k: c.ngh.ap[:, layer, jpost, k:k + 1]

        kb.dma("sp", out=hb[0].ap, in_=hview(hT_in, 0), writes=[hb[0]])
        for t in range(ntiles):
            h = hb[t % 2]
            if t + 1 < ntiles:
                kb.dma("sp", out=hb[(t + 1) % 2].ap, in_=hview(hT_in, t + 1), writes=[hb[(t + 1) % 2]])
            prenorm(kb, c, h, sq, ps_ss, rstd, u_t, gpre)

            def gu(j):
                p = pgu[j % 2]
                for k in range(NKC):
                    kb.op("pe", lambda e, k=k: e.matmul(p.ap[:, 0:TT], lhsT=wgb[:, k, j * 128:(j + 1) * 128], rhs=u_t[k].ap,
                                                        start=(k == 0), stop=(k == NKC - 1)),
                          reads=[u_t[k]], writes=[p], signal=False)
                for k in range(NKC):
                    kb.op("pe", lambda e, k=k: e.matmul(p.ap[:, TT:2 * TT], lhsT=wub[:, k, j * 128:(j + 1) * 128], rhs=u_t[k].ap,
                                                        start=(k == 0), stop=(k == NKC - 1)),
                          reads=[u_t[k]], writes=[p], signal=(k == NKC - 1))
                s = sg[j % 2]
                a = act[j % 3]
                kb.op("act", lambda e: e.activation(out=s.ap, in_=p.ap[:, 0:TT], func=AF.Silu), reads=[p], writes=[s])
                kb.op("dve", lambda e: e.tensor_tensor(out=a.ap, in0=s.ap, in1=p.ap[:, TT:2 * TT], op=ALU.mult), reads=[s, p], writes=[a])

            def down(j):
                a = act[j % 3]
                for i in range(NKC):
                    kb.op("pe", lambda e, i=i: e.matmul(psy[i][1], lhsT=wdb[:, j, i * 128:(i + 1) * 128], rhs=a.ap,
                                                        start=(j == 0 and i % 2 == 0), stop=(j == NFC - 1),
                                                        skip_group_check=True),
                          reads=[a], writes=[psy[i][0]], signal=(i == NKC - 1))

            gu(0)
            for j in range(1, NFC):
                gu(j)
                down(j - 1)
            down(NFC - 1)
            postnorm_residual(kb, c, psy, h, sq2_t, ps_ss2, rstd2, tmp_t, gpost)
            kb.dma("pool", out=hview(hT_out, t), in_=h.ap, reads=[h])
        kb.barrier()


def load_cast_simple(kb, stg, src2d, dst_fn, nrow_chunks, ncol, state, toks):
    half = 1408
    for k in range(nrow_chunks):
        for c0 in range(0, ncol, half):
            cw = min(half, ncol - c0)
            i = state[0]
            state[0] += 1
            st = stg[i % len(stg)]
            kb.dma("sp" if i % 2 == 0 else "pool", out=st.ap[:, :cw], in_=src2d[k * 128:(k + 1) * 128, c0:c0 + cw], writes=[st])
            ce = ("dve", "act")[i % 2]
            if ce == "act":
                kb.op("act", lambda e: e.copy(out=dst_fn(k, c0, cw), in_=st.ap[:, :cw]), reads=[st])
            else:
                kb.op("dve", lambda e: e.tensor_copy(out=dst_fn(k, c0, cw), in_=st.ap[:, :cw]), reads=[st])
            toks.append((ce, kb.cnt[ce]))


def qkv_phase(kb, c, hT_in, layer, fm, vspec, tabs, gcols, tok=TOK):
    ntiles = tok // TT
    with ExitStack() as es:
        nch = [f["ncols"] // 128 for f in fm]
        totch = sum(nch)
        wb = [kb.sb(es, [128, NKC, f["ncols"]], BF16, "wq") for f in fm]
        wsb = [kb.sb(es, [128, NKC, f["ncols"]], BF16, "wqs") if f["wsw"] is not None else None for f in fm]
        nv = vspec["nv"]
        wvb = kb.sb(es, [128, NKC, nv], BF16, "wv")
        stg = [Tile(kb.sb(es, [128, 1408], F32, "stg")[:]) for _ in range(3)]
        hb = [Tile(kb.sb(es, [128, NKC, TT], F32, "hb")[:]) for _ in range(2)]
        sq = Tile(kb.sb(es, [128, NKC, TT], BF16, "sq")[:])
        u_raw = kb.sb(es, [128, NKC, TT], BF16, "u")
        u_t = [Tile(u_raw[:, k, :]) for k in range(NKC)]
        rstd = Tile(kb.sb(es, [128, TT], F32, "rstd")[:])
        ntab = 0 if tabs is None else tabs.shape[0]
        tb = [Tile(kb.sb(es, [128, max(ntab, 1), TT], F32, "tab")[:]) for _ in range(2)]
        tbg = [Tile(kb.sb(es, [128, max(ntab, 1), TT], F32, "tabg")[:]) for _ in range(2)]
        gc = None
        if gcols is not None:
            gc = Tile(kb.sb(es, [128, gcols.shape[1]], F32, "gc")[:])
            kb.dma("sp", out=gc.ap, in_=gcols, writes=[gc])
        ostage = [[Tile(kb.sb(es, [128, n, TT], BF16, "ost")[:]) for n in nch] for _ in range(2)]
        vst = [Tile(kb.sb(es, [128, 512], BF16, "vst")[:]) for _ in range(2)]
        t1 = [Tile(kb.sb(es, [128, TT], F32, "t1")[:]) for _ in range(2)]
        t2 = [Tile(kb.sb(es, [128, TT], F32, "t2")[:]) for _ in range(2)]
        sqb = [Tile(kb.sb(es, [128, TT], BF16, "sqb")[:]) for _ in range(2)]
        rn = [Tile(kb.sb(es, [128, TT], F32, "rn")[:]) for _ in range(2)]
        ps_ss = Tile(kb.ps(es, [128, 512], F32, "pss")[:])
        pab = [Tile(kb.ps(es, [128, 512], F32, "pab")[:]) for _ in range(2)]
        pn = [Tile(kb.ps(es, [128, 512], F32, "pn")[:]) for _ in range(2)]
        pv = [Tile(kb.ps(es, [128, 512], F32, "pv")[:]) for _ in range(2)]

        toks = []
        state = [0]
        for fi, f in enumerate(fm):
            load_cast_simple(kb, stg, f["w"], lambda k, c0, cw, fi=fi: wb[fi][:, k, c0:c0 + cw], NKC, f["ncols"], state, toks)
            if f["wsw"] is not None:
                load_cast_simple(kb, stg, f["wsw"], lambda k, c0, cw, fi=fi: wsb[fi][:, k, c0:c0 + cw], NKC, f["ncols"], state, toks)
        load_cast_simple(kb, stg, vspec["w"], lambda k, c0, cw: wvb[:, k, c0:c0 + cw], NKC, nv, state, toks)
        wait_toks(kb, "pe", toks)

        gpre = lambda k: c.ng.ap[:, layer, 2, k:k + 1]
        kb.dma("sp", out=hb[0].ap, in_=hview(hT_in, 0), writes=[hb[0]])
        cnt = 0
        for t in range(ntiles):
            h = hb[t % 2]
            if t + 1 < ntiles:
                kb.dma("sp", out=hb[(t + 1) % 2].ap, in_=hview(hT_in, t + 1), writes=[hb[(t + 1) % 2]])
            tab = tb[t % 2]
            tabg = tbg[t % 2]
            if ntab:
                kb.dma("pool", out=tab.ap, in_=tabs.rearrange("n p t -> p n t")[:, :, t * TT:(t + 1) * TT], writes=[tab])
            prenorm(kb, c, h, sq, ps_ss, rstd, u_t, gpre)
            for fi, f in enumerate(fm):
                if f["mode"] == "normrope":
                    ci, si = f["tab"]
                    g0, g1 = f["g"]
                    kb.op("dve", lambda e: e.tensor_scalar_mul(out=tabg.ap[:, ci, :], in0=tab.ap[:, ci, :], scalar1=gc.ap[:, g0:g0 + 1]),
                          reads=[tab, gc], writes=[tabg])
                    kb.op("dve", lambda e: e.tensor_scalar_mul(out=tabg.ap[:, si, :], in0=tab.ap[:, si, :], scalar1=gc.ap[:, g1:g1 + 1]),
                          reads=[tab, gc], writes=[tabg])
            for fi, f in enumerate(fm):
                ost = ostage[t % 2][fi]
                for m in range(nch[fi]):
                    p = pab[cnt % 2]
                    x1 = t1[cnt % 2]
                    x2 = t2[cnt % 2]
                    has_sw = f["wsw"] is not None
                    for k in range(NKC):
                        kb.op("pe", lambda e, k=k: e.matmul(p.ap[:, 0:TT], lhsT=wb[fi][:, k, m * 128:(m + 1) * 128], rhs=u_t[k].ap,
                                                            start=(k == 0), stop=(k == NKC - 1)),
                              reads=[u_t[k]], writes=[p], signal=(k == NKC - 1 and not has_sw))
                    if has_sw:
                        for k in range(NKC):
                            kb.op("pe", lambda e, k=k: e.matmul(p.ap[:, TT:2 * TT], lhsT=wsb[fi][:, k, m * 128:(m + 1) * 128], rhs=u_t[k].ap,
                                                                start=(k == 0), stop=(k == NKC - 1)),
                                  reads=[u_t[k]], writes=[p], signal=(k == NKC - 1))
                    if f["mode"] == "plain":
                        sc = f.get("scale", 1.0)
                        kb.op("act", lambda e: e.activation(out=ost.ap[:, m, :], in_=p.ap[:, 0:TT], func=AF.Copy, scale=sc),
                              reads=[p], writes=[ost])
                    elif f["mode"] == "rope":
                        ci, si = f["tab"]
                        kb.op("dve", lambda e: e.tensor_tensor(out=x1.ap, in0=p.ap[:, 0:TT], in1=tab.ap[:, ci, :], op=ALU.mult),
                              reads=[p, tab], writes=[x1])
                        kb.op("dve", lambda e: e.tensor_tensor(out=x2.ap, in0=p.ap[:, TT:2 * TT], in1=tab.ap[:, si, :], op=ALU.mult),
                              reads=[p, tab], writes=[x2])
                        kb.op("pool", lambda e: e.tensor_tensor(out=ost.ap[:, m, :], in0=x1.ap, in1=x2.ap, op=ALU.add),
                              reads=[x1, x2], writes=[ost])
                    else:
                        ci, si = f["tab"]
                        sb_ = sqb[cnt % 2]
                        pnn = pn[cnt % 2]
                        r = rn[cnt % 2]
                        kb.op("act", lambda e: e.activation(out=sb_.ap, in_=p.ap[:, 0:TT], func=AF.Square), reads=[p], writes=[sb_])
                        kb.op("pe", lambda e: e.matmul(pnn.ap[:, 0:TT], lhsT=c.blk64.ap, rhs=sb_.ap, start=True, stop=True),
                              reads=[sb_, c.blk64], writes=[pnn])
                        rstd_from_ps(kb, pnn, r)
                        kb.op("dve", lambda e: e.tensor_tensor(out=x1.ap, in0=p.ap[:, 0:TT], in1=tabg.ap[:, ci, :], op=ALU.mult),
                              reads=[p, tabg], writes=[x1])
                        kb.op("dve", lambda e: e.tensor_tensor(out=x2.ap, in0=p.ap[:, TT:2 * TT], in1=tabg.ap[:, si, :], op=ALU.mult),
                              reads=[p, tabg], writes=[x2])
                        kb.op("pool", lambda e: e.tensor_tensor(out=x1.ap, in0=x1.ap, in1=x2.ap, op=ALU.add),
                              reads=[x1, x2], writes=[x1])
                        kb.op("pool", lambda e: e.tensor_tensor(out=ost.ap[:, m, :], in0=x1.ap, in1=r.ap, op=ALU.mult),
                              reads=[x1, r], writes=[ost])
                    cnt += 1
                kb.dma("sp", out=f["out"].rearrange("(m p) t -> p m t", p=128)[:, :, t * TT:(t + 1) * TT], in_=ost.ap, reads=[ost])
            vi = 0
            for ts in range(TT // 128):
                for cb in range(0, nv, 512):
                    cw = min(512, nv - cb)
                    p = pv[vi % 2]
                    vs = vst[vi % 2]
                    vi += 1
                    for k in range(NKC):
                        kb.op("pe", lambda e, k=k: e.matmul(p.ap[:, 0:cw], lhsT=u_t[k].ap[:, ts * 128:(ts + 1) * 128], rhs=wvb[:, k, cb:cb + cw],
                                                            start=(k == 0), stop=(k == NKC - 1)),
                              reads=[u_t[k]], writes=[p], signal=(k == NKC - 1))
                    kb.op("act", lambda e: e.copy(out=vs.ap[:, 0:cw], in_=p.ap[:, 0:cw]), reads=[p], writes=[vs])
                    kb.dma("pool", out=vspec["out"][t * TT + ts * 128:t * TT + (ts + 1) * 128, cb:cb + cw], in_=vs.ap[:, 0:cw], reads=[vs])
        kb.barrier()


QT_ = 512


def attn_phase_half(kb, c, QT, AT, spec, tok=TOK, skeys=SEQ):
    kind = spec["kind"]
    dv = spec["dv"]
    n_out = spec["n_out"]
    comps = spec["comps"]
    nqt = tok // QT_
    nkc_all = skeys // 128
    with ExitStack() as es:
        NB = 2
        ktb = [[Tile(kb.sb(es, [64, skeys], BF16, "kt")[:]) for _ in range(comps)] for _ in range(NB)]
        qtb = [[Tile(kb.sb(es, [64, tok], BF16, "qt")[:]) for _ in range(comps)] for _ in range(NB)]
        vb = [Tile(kb.sb(es, [128, nkc_all, dv], BF16, "v")[:]) for _ in range(NB)]
        pbuf = [Tile(kb.sb(es, [128, QT_], BF16, "p")[:]) for _ in range(4)]
        rs = [Tile(kb.sb(es, [128, QT_], F32, "rs")[:]) for _ in range(2)]
        on = [[Tile(kb.sb(es, [128, QT_], F32, "on")[:]) for _ in range(comps)] for _ in range(2)]
        ost = [Tile(kb.sb(es, [128, QT_], BF16, "ost")[:]) for _ in range(2)]
        psS = [Tile(kb.ps(es, [128, 512], F32, "pS")[:]) for _ in range(3)]
        psO = [Tile(kb.ps(es, [128, 512], F32, "pO")[:]) for _ in range(2)]
        psM = [Tile(kb.ps(es, [128, 512], F32, "pM")[:]) for _ in range(2)]
        if kind == "diff":
            psN = Tile(kb.ps(es, [128, 512], F32, "pN")[:])
            od = [Tile(kb.sb(es, [128, QT_], F32, "od")[:]) for _ in range(2)]
            sqd = [Tile(kb.sb(es, [128, QT_], BF16, "sqd")[:]) for _ in range(2)]
            rd = [Tile(kb.sb(es, [128, QT_], F32, "rd")[:]) for _ in range(2)]
            lv = Tile(kb.sb(es, [128, 256], F32, "lv")[:])
            lt = Tile(kb.sb(es, [128, 64], F32, "lt")[:])
            ls = Tile(kb.sb(es, [128, 4], F32, "ls")[:])
            sgc = Tile(kb.sb(es, [128, 2], F32, "sgc")[:])
            kb.dma("sp", out=lv.ap, in_=spec["lam"], writes=[lv])
            kb.dma("sp", out=sgc.ap[:, 0:1], in_=spec["subg"], writes=[sgc])
            for i in range(2):
                kb.op("dve", lambda e: e.tensor_tensor(out=lt.ap, in0=lv.ap[:, (2 * i) * 64:(2 * i + 1) * 64],
                                                       in1=lv.ap[:, (2 * i + 1) * 64:(2 * i + 2) * 64], op=ALU.mult), reads=[lv], writes=[lt])
                kb.op("dve", lambda e: e.reduce_sum(out=ls.ap[:, i:i + 1], in_=lt.ap, axis=mybir.AxisListType.X), reads=[lt], writes=[ls])
            kb.op("act", lambda e: e.activation(out=ls.ap[:, 0:2], in_=ls.ap[:, 0:2], func=AF.Exp), reads=[ls], writes=[ls])
            kb.op("dve", lambda e: e.tensor_tensor(out=ls.ap[:, 2:3], in0=ls.ap[:, 1:2], in1=ls.ap[:, 0:1], op=ALU.subtract), reads=[ls], writes=[ls])
            kb.op("dve", lambda e: e.tensor_scalar_add(out=ls.ap[:, 2:3], in0=ls.ap[:, 2:3], scalar1=-spec["lam_init"]), reads=[ls], writes=[ls])
            kb.op("dve", lambda e: e.tensor_scalar_mul(out=sgc.ap[:, 1:2], in0=sgc.ap[:, 0:1], scalar1=1.0 - spec["lam_init"]), reads=[sgc], writes=[sgc])
        if kind == "na":
            ident = Tile(kb.sb(es, [128, 128], BF16, "ident")[:])
            identf = Tile(kb.sb(es, [128, 128], F32, "identf")[:])
            kb.op("pool", lambda e: e.memset(identf.ap, 0.0), writes=[identf])
            kb.op("pool", lambda e: e.affine_select(out=identf.ap, in_=identf.ap, pattern=[[-1, 128]], compare_op=ALU.not_equal,
                                                    fill=1.0, base=0, channel_multiplier=1), reads=[identf], writes=[identf])
            kb.op("pool", lambda e: e.tensor_copy(out=ident.ap, in_=identf.ap), reads=[identf], writes=[ident])
            tabf = [Tile(kb.sb(es, [128, 8, 512], F32, "tabf")[:]) for _ in range(3)]
            tabb = [[Tile(kb.sb(es, [128, 8, 512], BF16, "tabb")[:]) for _ in range(3)] for _ in range(NB)]

        def load_head(oh, slot):
            qi = 0
            for cp in range(comps):
                qr = spec["qrow"](oh, cp)
                for (c0, ncol, src) in spec["kload"](oh, cp):
                    kb.dma(("sp", "pool")[qi % 2], out=ktb[slot][cp].ap[:, c0:c0 + ncol], in_=src, writes=[ktb[slot][cp]])
                    qi += 1
                kb.dma("pool", out=qtb[slot][cp].ap, in_=QT[qr * 64:(qr + 1) * 64, :], writes=[qtb[slot][cp]])
            for (k0, nk_, src) in spec["vload"](oh):
                kb.dma(("sp", "pool")[qi % 2], out=vb[slot].ap[:, k0:k0 + nk_, :], in_=src, writes=[vb[slot]])
                qi += 1
            if kind == "na":
                for cl in range(3):
                    kb.dma("sp", out=tabf[cl].ap, in_=spec["tab"][cl, oh], writes=[tabf[cl]])
                    kb.op("pool", lambda e: e.tensor_copy(out=tabb[slot][cl].ap, in_=tabf[cl].ap), reads=[tabf[cl]], writes=[tabb[slot][cl]])

        steps = []
        for oh in range(n_out):
            for qt in range(nqt):
                st, n = spec["band"](qt)
                for cp in range(comps):
                    for m in range(n):
                        steps.append((oh, qt, cp, st + m, m, m == 0, m == n - 1))

        def qk(n):
            oh, qt, cp, kc, m, first, last = steps[n]
            slot = oh % NB
            p = psS[n % 3]
            has_b = kind == "na"
            kb.op("pe", lambda e: e.matmul(p.ap, lhsT=ktb[slot][cp].ap[:, kc * 128:(kc + 1) * 128], rhs=qtb[slot][cp].ap[:, qt * QT_:(qt + 1) * QT_],
                                           start=True, stop=not has_b),
                  reads=[ktb[slot][cp], qtb[slot][cp]], writes=[p], signal=not has_b)
            if has_b:
                tb_ = tabb[slot][spec["cls"](qt)]
                kb.op("pe", lambda e: e.matmul(p.ap, lhsT=ident.ap, rhs=tb_.ap[:, m, :], start=False, stop=True),
                      reads=[tb_, ident], writes=[p])

        fin_i = [0]

        def pvstep(n):
            oh, qt, cp, kc, m, first, last = steps[n]
            slot = oh % NB
            p = psS[n % 3]
            pb = pbuf[n % 4]
            grp = (oh * nqt + qt) * comps + cp
            po = psO[grp % 2]
            pm = psM[grp % 2]
            kb.op("act", lambda e: e.activation(out=pb.ap, in_=p.ap, func=AF.Exp), reads=[p], writes=[pb])
            kb.op("pe", lambda e: e.matmul(po.ap[0:dv, :], lhsT=vb[slot].ap[:, kc, :], rhs=pb.ap, start=first, stop=last),
                  reads=[vb[slot], pb], writes=[po], signal=last)
            kb.op("pe", lambda e: e.matmul(pm.ap[0:dv, :], lhsT=c.ones_1.ap[:, 0:dv], rhs=pb.ap, start=first, stop=last),
                  reads=[c.ones_1, pb], writes=[pm], signal=True)
            if not last:
                return
            g2 = (oh * nqt + qt) % 2
            r = rs[grp % 2]
            o_n = on[g2][cp]
            kb.op("dve", lambda e: e.reciprocal(out=r.ap[0:dv, :], in_=pm.ap[0:dv, :]), reads=[pm], writes=[r])
            if kind != "diff":
                os_ = ost[g2]
                kb.op("dve", lambda e: e.tensor_tensor(out=os_.ap[0:dv, :], in0=po.ap[0:dv, :], in1=r.ap[0:dv, :], op=ALU.mult),
                      reads=[po, r], writes=[os_])
                kb.dma("pool", out=AT[oh * dv:(oh + 1) * dv, qt * QT_:(qt + 1) * QT_], in_=os_.ap[0:dv, :], reads=[os_])
                return
            kb.op("dve", lambda e: e.tensor_tensor(out=o_n.ap, in0=po.ap, in1=r.ap, op=ALU.mult), reads=[po, r], writes=[o_n])
            if cp != comps - 1:
                return
            o = od[g2]
            s_ = sqd[g2]
            rr = rd[g2]
            os_ = ost[g2]
            kb.op("dve", lambda e: e.scalar_tensor_tensor(out=o.ap, in0=on[g2][1].ap, scalar=ls.ap[:, 2:3], in1=on[g2][0].ap,
                                                          op0=ALU.mult, op1=ALU.add), reads=[on[g2][0], on[g2][1], ls], writes=[o])
            kb.op("act", lambda e: e.activation(out=s_.ap, in_=o.ap, func=AF.Square), reads=[o], writes=[s_])
            kb.op("pe", lambda e: e.matmul(psN.ap, lhsT=c.ones_128.ap, rhs=s_.ap, start=True, stop=True), reads=[s_, c.ones_128], writes=[psN])
            rstd_from_ps(kb, psN, rr, tt=QT_)
            kb.op("dve", lambda e: e.scalar_tensor_tensor(out=os_.ap, in0=o.ap, scalar=sgc.ap[:, 1:2], in1=rr.ap, op0=ALU.mult, op1=ALU.mult),
                  reads=[o, rr, sgc], writes=[os_])
            kb.dma("pool", out=AT[oh * 128:(oh + 1) * 128, qt * QT_:(qt + 1) * QT_], in_=os_.ap, reads=[os_])

        load_head(0, 0)
        N = len(steps)
        cur = -1
        for n in range(N):
            oh = steps[n][0]
            if oh != cur:
                cur = oh
                if oh + 1 < n_out:
                    load_head(oh + 1, (oh + 1) % NB)
            if n == 0:
                qk(0)
                if N > 1:
                    qk(1)
            if n + 2 < N:
                qk(n + 2)
            pvstep(n)
        kb.barrier()


def attn_phase(kb, c, QT, AT, spec, tok=TOK, skeys=SEQ):
    kind = spec["kind"]
    if kind == "diff" and os.environ.get("KDIFF_HALF"):
        return attn_phase_half(kb, c, QT, AT, spec, tok=tok, skeys=skeys)
    dv = spec["dv"]
    n_out = spec["n_out"]
    comps = spec["comps"]
    nqt = tok // QT_
    nkc_all = skeys // 128
    with ExitStack() as es:
        NB = 2
        ktb = [Tile(kb.sb(es, [128, skeys], BF16, "kt")[:]) for _ in range(NB)]
        qtb = [[Tile(kb.sb(es, [128, tok], BF16, "qt")[:]) for _ in range(comps)] for _ in range(NB)]
        vb = [Tile(kb.sb(es, [128, nkc_all, 128], BF16, "v")[:]) for _ in range(NB)]
        if dv == 64:
            for b in range(NB):
                kb.op("pool", lambda e: e.memset(vb[b].ap[:, :, 64:128], 1.0), writes=[vb[b]])
        pbuf = [Tile(kb.sb(es, [128, QT_], BF16, "p")[:]) for _ in range(4)]
        rs = [Tile(kb.sb(es, [128, QT_], F32, "rs")[:]) for _ in range(2)]
        on = [[Tile(kb.sb(es, [128, QT_], F32, "on")[:]) for _ in range(comps)] for _ in range(2)]
        ost = [Tile(kb.sb(es, [128, QT_], BF16, "ost")[:]) for _ in range(2)]
        psS = [Tile(kb.ps(es, [128, 512], F32, "pS")[:]) for _ in range(3)]
        psO = [Tile(kb.ps(es, [128, 512], F32, "pO")[:]) for _ in range(2)]
        psM = [Tile(kb.ps(es, [128, 512], F32, "pM")[:]) for _ in range(2)]
        if kind == "diff":
            psN = Tile(kb.ps(es, [128, 512], F32, "pN")[:])
            od = [Tile(kb.sb(es, [128, QT_], F32, "od")[:]) for _ in range(2)]
            sqd = [Tile(kb.sb(es, [128, QT_], BF16, "sqd")[:]) for _ in range(2)]
            rd = [Tile(kb.sb(es, [128, QT_], F32, "rd")[:]) for _ in range(2)]
            lv = Tile(kb.sb(es, [128, 256], F32, "lv")[:])
            lt = Tile(kb.sb(es, [128, 64], F32, "lt")[:])
            ls = Tile(kb.sb(es, [128, 4], F32, "ls")[:])
            sgc = Tile(kb.sb(es, [128, 2], F32, "sgc")[:])
            kb.dma("sp", out=lv.ap, in_=spec["lam"], writes=[lv])
            kb.dma("sp", out=sgc.ap[:, 0:1], in_=spec["subg"], writes=[sgc])
            for i in range(2):
                kb.op("dve", lambda e: e.tensor_tensor(out=lt.ap, in0=lv.ap[:, (2 * i) * 64:(2 * i + 1) * 64],
                                                       in1=lv.ap[:, (2 * i + 1) * 64:(2 * i + 2) * 64], op=ALU.mult), reads=[lv], writes=[lt])
                kb.op("dve", lambda e: e.reduce_sum(out=ls.ap[:, i:i + 1], in_=lt.ap, axis=mybir.AxisListType.X), reads=[lt], writes=[ls])
            kb.op("act", lambda e: e.activation(out=ls.ap[:, 0:2], in_=ls.ap[:, 0:2], func=AF.Exp), reads=[ls], writes=[ls])
            kb.op("dve", lambda e: e.tensor_tensor(out=ls.ap[:, 2:3], in0=ls.ap[:, 1:2], in1=ls.ap[:, 0:1], op=ALU.subtract), reads=[ls], writes=[ls])
            kb.op("dve", lambda e: e.tensor_scalar_add(out=ls.ap[:, 2:3], in0=ls.ap[:, 2:3], scalar1=-spec["lam_init"]), reads=[ls], writes=[ls])
            kb.op("dve", lambda e: e.tensor_scalar_mul(out=sgc.ap[:, 1:2], in0=sgc.ap[:, 0:1], scalar1=1.0 - spec["lam_init"]), reads=[sgc], writes=[sgc])
        if kind == "na":
            ident = Tile(kb.sb(es, [128, 128], BF16, "ident")[:])
            identf = Tile(kb.sb(es, [128, 128], F32, "identf")[:])
            kb.op("pool", lambda e: e.memset(identf.ap, 0.0), writes=[identf])
            kb.op("pool", lambda e: e.affine_select(out=identf.ap, in_=identf.ap, pattern=[[-1, 128]], compare_op=ALU.not_equal,
                                                    fill=1.0, base=0, channel_multiplier=1), reads=[identf], writes=[identf])
            kb.op("pool", lambda e: e.tensor_copy(out=ident.ap, in_=identf.ap), reads=[identf], writes=[ident])
            tabf = [Tile(kb.sb(es, [128, 8, 512], F32, "tabf")[:]) for _ in range(3)]
            tabb = [[Tile(kb.sb(es, [128, 8, 512], BF16, "tabb")[:]) for _ in range(3)] for _ in range(NB)]

        def load_head(oh, slot):
            qi = 0
            for (c0, ncol, src) in spec["kload"](oh):
                kb.dma(("sp", "pool")[qi % 2], out=ktb[slot].ap[:, c0:c0 + ncol], in_=src, writes=[ktb[slot]])
                qi += 1
            for cp in range(comps):
                qr = spec["qrow"](oh, cp)
                hs = spec["khalf"](oh, cp)
                kb.op("pool", lambda e: e.memset(qtb[slot][cp].ap[(1 - hs) * 64:(2 - hs) * 64, :], 0.0), writes=[qtb[slot][cp]])
                kb.dma("pool", out=qtb[slot][cp].ap[hs * 64:(hs + 1) * 64, :], in_=QT[qr * 64:(qr + 1) * 64, :], writes=[qtb[slot][cp]])
            for (k0, nk_, src) in spec["vload"](oh):
                kb.dma(("sp", "pool")[qi % 2], out=vb[slot].ap[:, k0:k0 + nk_, 0:dv], in_=src, writes=[vb[slot]])
                qi += 1
            if kind == "na":
                for cl in range(3):
                    kb.dma("sp", out=tabf[cl].ap, in_=spec["tab"][cl, oh], writes=[tabf[cl]])
                    kb.op("pool", lambda e: e.tensor_copy(out=tabb[slot][cl].ap, in_=tabf[cl].ap), reads=[tabf[cl]], writes=[tabb[slot][cl]])

        steps = []
        for oh in range(n_out):
            for qt in range(nqt):
                st, n = spec["band"](qt)
                for cp in range(comps):
                    for m in range(n):
                        steps.append((oh, qt, cp, st + m, m, m == 0, m == n - 1))

        def qk(n):
            oh, qt, cp, kc, m, first, last = steps[n]
            slot = oh % NB
            p = psS[n % 3]
            has_b = kind == "na"
            kb.op("pe", lambda e: e.matmul(p.ap, lhsT=ktb[slot].ap[:, kc * 128:(kc + 1) * 128], rhs=qtb[slot][cp].ap[:, qt * QT_:(qt + 1) * QT_],
                                           start=True, stop=not has_b),
                  reads=[ktb[slot], qtb[slot][cp]], writes=[p], signal=not has_b)
            if has_b:
                tb_ = tabb[slot][spec["cls"](qt)]
                kb.op("pe", lambda e: e.matmul(p.ap, lhsT=ident.ap, rhs=tb_.ap[:, m, :], start=False, stop=True),
                      reads=[tb_, ident], writes=[p])

        fin_i = [0]

        def pvstep(n):
            oh, qt, cp, kc, m, first, last = steps[n]
            slot = oh % NB
            p = psS[n % 3]
            pb = pbuf[n % 4]
            grp = (oh * nqt + qt) * comps + cp
            po = psO[grp % 2]
            pm = psM[grp % 2]
            kb.op("act", lambda e: e.activation(out=pb.ap, in_=p.ap, func=AF.Exp), reads=[p], writes=[pb])
            if dv == 64:
                kb.op("pe", lambda e: e.matmul(po.ap, lhsT=vb[slot].ap[:, kc, :], rhs=pb.ap, start=first, stop=last),
                      reads=[vb[slot], pb], writes=[po], signal=True)
            else:
                kb.op("pe", lambda e: e.matmul(po.ap, lhsT=vb[slot].ap[:, kc, :], rhs=pb.ap, start=first, stop=last),
                      reads=[vb[slot], pb], writes=[po], signal=last)
                kb.op("pe", lambda e: e.matmul(pm.ap, lhsT=c.ones_1.ap, rhs=pb.ap, start=first, stop=last),
                      reads=[c.ones_1, pb], writes=[pm], signal=True)
            if not last:
                return
            g2 = (oh * nqt + qt) % 2
            r = rs[grp % 2]
            o_n = on[g2][cp]
            if kind != "diff":
                os_ = ost[g2]
                kb.op("dve", lambda e: e.reciprocal(out=r.ap[64:128, :], in_=po.ap[64:128, :]), reads=[po], writes=[r])
                kb.op("dve", lambda e: e.tensor_tensor(out=os_.ap[0:64, :], in0=po.ap[0:64, :], in1=r.ap[64:128, :], op=ALU.mult),
                      reads=[po, r], writes=[os_])
                kb.dma("pool", out=AT[oh * dv:(oh + 1) * dv, qt * QT_:(qt + 1) * QT_], in_=os_.ap[0:dv, :], reads=[os_])
                return
            kb.op("dve", lambda e: e.reciprocal(out=r.ap, in_=pm.ap), reads=[pm], writes=[r])
            kb.op("dve", lambda e: e.tensor_tensor(out=o_n.ap, in0=po.ap, in1=r.ap, op=ALU.mult), reads=[po, r], writes=[o_n])
            if cp != comps - 1:
                return
            o = od[g2]
            s_ = sqd[g2]
            rr = rd[g2]
            os_ = ost[g2]
            kb.op("dve", lambda e: e.scalar_tensor_tensor(out=o.ap, in0=on[g2][1].ap, scalar=ls.ap[:, 2:3], in1=on[g2][0].ap,
                                                          op0=ALU.mult, op1=ALU.add), reads=[on[g2][0], on[g2][1], ls], writes=[o])
            kb.op("act", lambda e: e.activation(out=s_.ap, in_=o.ap, func=AF.Square), reads=[o], writes=[s_])
            kb.op("pe", lambda e: e.matmul(psN.ap, lhsT=c.ones_128.ap, rhs=s_.ap, start=True, stop=True), reads=[s_, c.ones_128], writes=[psN])
            rstd_from_ps(kb, psN, rr, tt=QT_)
            kb.op("dve", lambda e: e.scalar_tensor_tensor(out=os_.ap, in0=o.ap, scalar=sgc.ap[:, 1:2], in1=rr.ap, op0=ALU.mult, op1=ALU.mult),
                  reads=[o, rr, sgc], writes=[os_])
            kb.dma("pool", out=AT[oh * 128:(oh + 1) * 128, qt * QT_:(qt + 1) * QT_], in_=os_.ap, reads=[os_])

        load_head(0, 0)
        N = len(steps)
        cur = -1
        for n in range(N):
            oh = steps[n][0]
            if oh != cur:
                cur = oh
                if oh + 1 < n_out:
                    load_head(oh + 1, (oh + 1) % NB)
            if n == 0:
                qk(0)
                if N > 1:
                    qk(1)
            if n + 2 < N:
                qk(n + 2)
            pvstep(n)
        kb.barrier()


def outproj_phase(kb, c, hT_in, hT_out, AT, wo, layer, tok=TOK):
    ntiles = tok // TT
    with ExitStack() as es:
        wob = kb.sb(es, [128, NKC, D], BF16, "wo")
        stg = [Tile(kb.sb(es, [128, 1408], F32, "stg")[:]) for _ in range(3)]
        hb = [Tile(kb.sb(es, [128, NKC, TT], F32, "hb")[:]) for _ in range(2)]
        ab = [Tile(kb.sb(es, [128, NKC, TT], BF16, "ab")[:]) for _ in range(2)]
        sq2_raw = kb.sb(es, [128, NKC, TT], BF16, "sq2")
        sq2_t = [Tile(sq2_raw[:, i, :]) for i in range(NKC)]
        tmp_t = [Tile(kb.sb(es, [128, TT], F32, "tmp")[:]) for _ in range(2)]
        rstd2 = Tile(kb.sb(es, [128, TT], F32, "rstd2")[:])
        psyb = [Tile(kb.ps(es, [128, 512], F32, "py")[:]) for _ in range(4)]
        ps_ss2 = Tile(kb.ps(es, [128, 512], F32, "pss2")[:])
        psy = [(psyb[i // 2], psyb[i // 2].ap[:, (i % 2) * TT:(i % 2 + 1) * TT]) for i in range(NKC)]
        toks = []
        load_cast_simple(kb, stg, wo, lambda k, c0, cw: wob[:, k, c0:c0 + cw], NKC, D, [0], toks)
        wait_toks(kb, "pe", toks)
        gpost = lambda k: c.ng.ap[:, layer, 3, k:k + 1]
        for t in range(ntiles):
            h = hb[t % 2]
            a = ab[t % 2]
            kb.dma("sp", out=h.ap, in_=hview(hT_in, t), writes=[h])
            kb.dma("pool", out=a.ap, in_=hview(AT, t), writes=[a])
            for k in range(NKC):
                for i in range(NKC):
                    kb.op("pe", lambda e: e.matmul(psy[i][1], lhsT=wob[:, k, i * 128:(i + 1) * 128], rhs=a.ap[:, k, :],
                                                   start=(k == 0 and i % 2 == 0), stop=(k == NKC - 1), skip_group_check=True),
                          reads=[a], writes=[psy[i][0]], signal=(k == NKC - 1))
            postnorm_residual(kb, c, psy, h, sq2_t, ps_ss2, rstd2, tmp_t, gpost)
            kb.dma("pool", out=hview(hT_out, t), in_=h.ap, reads=[h])
        kb.barrier()


MIX = ["diff", "na", "gqa", "diff"]
MIXJ = [0, 0, 0, 1]
BF = ml_dtypes.bfloat16


def _partner_perm(kind):
    d = np.arange(64)
    if kind == "diff":
        return np.where(d < 8, d + 8, np.where(d < 16, d - 8, d))
    return np.where(d % 32 < 16, d + 16, d - 16)


def rope_tables(kind, half, tok=TOK):
    pos = (half * tok + np.arange(tok)).astype(np.float32)
    C = np.ones((64, tok), np.float32)
    S = np.zeros((64, tok), np.float32)
    if kind == "diff":
        inv = (np.float32(500000.0) ** (-(np.arange(0, 16, 2, dtype=np.float32)) / np.float32(16))).astype(np.float32)
        ang = (pos[None, :] * inv[:, None]).astype(np.float32)
        C[0:8] = np.cos(ang); C[8:16] = np.cos(ang)
        S[0:8] = -np.sin(ang); S[8:16] = np.sin(ang)
    else:
        inv = (np.float32(10000.0) ** (-(np.arange(0, 32, 2, dtype=np.float32)) / np.float32(32))).astype(np.float32)
        ip = (half * tok + np.arange(tok))
        ar = ((ip // 64).astype(np.float32)[None, :] * inv[:, None]).astype(np.float32)
        ac = ((ip % 64).astype(np.float32)[None, :] * inv[:, None]).astype(np.float32)
        C[0:16] = np.cos(ar); C[16:32] = np.cos(ar); C[32:48] = np.cos(ac); C[48:64] = np.cos(ac)
        S[0:16] = -np.sin(ar); S[16:32] = np.sin(ar); S[32:48] = -np.sin(ac); S[48:64] = np.sin(ac)
    C2 = np.concatenate([C, C], 0)
    S2 = np.concatenate([S, S], 0)
    sc = np.float32(0.125)
    return np.ascontiguousarray(np.stack([C2 * sc, S2 * sc, C2, S2]).astype(np.float32))


def na_tables(rpb, half):
    out = np.empty((3, 16, 128, 8, 512), np.float32)
    p = np.arange(128)
    q = np.arange(512)
    R0s = [0, 56, 8] if half == 0 else [64, 120, 72]
    for cls in range(3):
        R0 = R0s[cls]
        r = R0 + q // 64
        cq = q % 64
        r0 = np.clip(r - 4, 0, 120)
        c0 = np.clip(cq - 8, 0, 48)
        for m in range(8):
            kr = R0 - 4 + 2 * m + p // 64
            kc = p % 64
            inwin = ((kr[:, None] >= r0[None, :]) & (kr[:, None] < r0[None, :] + 8) &
                     (kc[:, None] >= c0[None, :]) & (kc[:, None] < c0[None, :] + 16))
            dr = np.clip(kr[:, None] - r[None, :] + 7, 0, 14)
            dc = np.clip(kc[:, None] - cq[None, :] + 15, 0, 30)
            g = rpb[:, dr, dc]
            out[cls, :, :, m, :] = np.where(inwin[None], g, np.float32(-30000.0))
    return out


NA_ROWS = 80
NA_KEYS = NA_ROWS * 64


def na_band(qt):
    return 4 * qt, 8


def na_cls(qt):
    return 0 if qt == 0 else (1 if qt == TOK // QT_ - 1 else 2)


def mixer_dims(kind):
    if kind == "gqa":
        return 1024, 256, 256
    return 1024, 1024, 1024


KP_ROWS = 256


def v_tpp(nv):
    return (2 * 1024 * 1024) // (nv * 2)


def vview(ap2d, c0, dv):
    return ap2d.rearrange("(kc p) e -> p kc e", p=128)[:, :, c0:c0 + dv]


def attn_spec(kind, layer, dr):
    GK, GV, KT, V = dr["GK"], dr["GV"], dr["KT"], dr["V"]
    nq, nk, nv = mixer_dims(kind)
    tpp = v_tpp(nv)

    def kload_full(pair):
        piece, r0 = (pair * 128) // KP_ROWS, (pair * 128) % KP_ROWS
        return [(rank * TOK, TOK, GK[piece, rank, r0:r0 + 128, :]) for rank in range(2)]

    def vload_full(vc, dv):
        res = []
        for rank in range(2):
            for piece in range(TOK // tpp):
                res.append(((rank * TOK + piece * tpp) // 128, tpp // 128, vview(GV[piece, rank], vc, dv)))
        return res

    if kind == "diff":
        def kload64(kr):
            piece, r0 = (kr * 64) // KP_ROWS, (kr * 64) % KP_ROWS
            return [(rank * TOK, TOK, GK[piece, rank, r0:r0 + 64, :]) for rank in range(2)]
        return dict(kind="diff", n_out=8, comps=2, dv=128, qrow=lambda oh, cp: oh * 2 + cp, khalf=lambda oh, cp: cp,
                    kload=(lambda oh: kload_full(oh)) if not os.environ.get("KDIFF_HALF") else (lambda oh, cp: kload64(oh * 2 + cp)),
                    vload=lambda oh: vload_full(oh * 128, 128),
                    band=lambda qt: (0, SEQ // 128), lam=dr["lam"], subg=dr["subg"],
                    lam_init=0.8 - 0.6 * float(np.exp(-0.3 * layer)))
    if kind == "gqa":
        return dict(kind="gqa", n_out=16, comps=1, dv=64, qrow=lambda oh, cp: oh, khalf=lambda oh, cp: (oh // 4) % 2,
                    kload=lambda oh: kload_full(oh // 8), vload=lambda oh: vload_full((oh // 4) * 64, 64),
                    band=lambda qt: (0, SEQ // 128))
    return dict(kind="na", n_out=16, comps=1, dv=64, qrow=lambda oh, cp: oh, khalf=lambda oh, cp: oh % 2,
                kload=lambda oh: [(0, 256, GK[0, 0, (oh // 2) * 128:(oh // 2 + 1) * 128, 768:1024]),
                                  (256, TOK, KT[(oh // 2) * 128:(oh // 2 + 1) * 128, :]),
                                  (256 + TOK, 768, GK[0, 1, (oh // 2) * 128:(oh // 2 + 1) * 128, 0:768])],
                vload=lambda oh: [(0, 2, vview(GV[0, 0, 768:1024, :], oh * 64, 64)),
                                  (2, TOK // 128, vview(V, oh * 64, 64)),
                                  (2 + TOK // 128, 6, vview(GV[0, 1, 0:768, :], oh * 64, 64))],
                band=na_band, cls=na_cls, tab=dr["natab"])


def emit_A(kb, c, dr, layer, h_src):
    kind = MIX[layer]
    nq, nk, nv = mixer_dims(kind)
    ffn_phase(kb, c, h_src, dr["h"], dr["wg1"], dr["wu1"], dr["wd1"], layer, 0, 1)
    w = dr["w_in"]
    if kind == "na":
        fm = [dict(w=w[:, 0:nq], wsw=None, out=dr["QT"], ncols=nq, mode="plain", scale=0.125),
              dict(w=w[:, nq:nq + nk], wsw=None, out=dr["KT"], ncols=nk, mode="plain", scale=1.0)]
        tabs, gcols = None, None
    else:
        mode = "rope" if kind == "diff" else "normrope"
        wsw = dr["w_sw"]
        fm = [dict(w=w[:, 0:nq], wsw=wsw[:, 0:nq], out=dr["QT"], ncols=nq, mode=mode, tab=(0, 1), g=(0, 1)),
              dict(w=w[:, nq:nq + nk], wsw=wsw[:, nq:nq + nk], out=dr["KT"], ncols=nk, mode=mode, tab=(2, 3), g=(2, 3))]
        tabs = dr["tabs"]
        gcols = dr["gcols"] if kind == "gqa" else None
    qkv_phase(kb, c, dr["h"], layer, fm, dict(w=w[:, nq + nk:nq + nk + nv], out=dr["V"], nv=nv), tabs, gcols)


def emit_X(kb, c, dr, layer):
    kind = MIX[layer]
    nq, nk, nv = mixer_dims(kind)
    if kind == "na":
        KT, V, KH, VH = dr["KT"], dr["V"], dr["KH"], dr["VH"]
        kb.dma("sp", out=KH[:, 0:768], in_=KT[:, 0:768])
        kb.dma("pool", out=KH[:, 768:1024], in_=KT[:, TOK - 256:TOK])
        kb.dma("sp", out=VH[0:768, :], in_=V[0:768, :])
        kb.dma("pool", out=VH[768:1024, :], in_=V[TOK - 256:TOK, :])
        kb.barrier()
        kb.allgather_pairs(KH, dr["GK"][0].rearrange("r p t -> (r p) t"))
        kb.allgather_pairs(VH, dr["GV"][0].rearrange("r p t -> (r p) t"))
    else:
        tpp = v_tpp(nv)
        for i in range(nk // KP_ROWS):
            kb.allgather_pairs(dr["KT"][i * KP_ROWS:(i + 1) * KP_ROWS, :], dr["GK"][i].rearrange("r p t -> (r p) t"))
        for i in range(TOK // tpp):
            kb.allgather_pairs(dr["V"][i * tpp:(i + 1) * tpp, :], dr["GV"][i].rearrange("r p t -> (r p) t"))
    kb.barrier()


def emit_B(kb, c, dr, layer, h_src):
    kind = MIX[layer]
    spec = attn_spec(kind, layer, dr)
    skeys = NA_KEYS if kind == "na" else SEQ
    attn_phase(kb, c, dr["QT"], dr["AT"], spec, skeys=skeys)
    if "O" in os.environ.get("KSKIP", ""):
        return
    outproj_phase(kb, c, h_src, dr["h"], dr["AT"], dr["wo"], layer)
    ffn_phase(kb, c, dr["h"], dr["h"], dr["wg2"], dr["wu2"], dr["wd2"], layer, 4, 5)


def declare(nc, name, shape, dt, kind):
    return nc.dram_tensor(name, list(shape), dt, kind=kind).ap()


def build_fused(layers=(0, 1, 2, 3)):
    nc = bass.Bass("TRN2", target_bir_lowering=False)
    ins = []

    def d_in(name, shape, dt=F32):
        ins.append(name)
        return declare(nc, name, shape, dt, "ExternalInput")

    def d_int(name, shape, dt=BF16):
        return declare(nc, name, shape, dt, "Internal")

    h_in = d_in("h_in", [D, TOK])
    ng = d_in("ng", [128, 4, 6, NKC])
    h = declare(nc, "h", [D, TOK], F32, "ExternalOutput")
    drs = []
    for layer in layers:
        kind = MIX[layer]
        nq, nk, nv = mixer_dims(kind)
        dr = {"h": h}
        sfx = "_L%d" % layer
        dr["wg1"] = d_in("wg1" + sfx, [D, DFF]); dr["wu1"] = d_in("wu1" + sfx, [D, DFF]); dr["wd1"] = d_in("wd1" + sfx, [DFF, D])
        dr["w_in"] = d_in("w_in" + sfx, [D, nq + nk + nv])
        if kind != "na":
            dr["w_sw"] = d_in("w_sw" + sfx, [D, nq + nk])
            dr["tabs"] = d_in("tabs" + sfx, [4, 128, TOK])
        if kind == "gqa":
            dr["gcols"] = d_in("gcols" + sfx, [128, 4])
        dr["wo"] = d_in("wo" + sfx, [D, D])
        dr["wg2"] = d_in("wg2" + sfx, [D, DFF]); dr["wu2"] = d_in("wu2" + sfx, [D, DFF]); dr["wd2"] = d_in("wd2" + sfx, [DFF, D])
        if kind == "diff":
            dr["lam"] = d_in("lam" + sfx, [128, 256]); dr["subg"] = d_in("subg" + sfx, [128, 1])
        if kind == "na":
            dr["natab"] = d_in("natab" + sfx, [3, 16, 128, 8, 512])
        dr["QT"] = d_int("QT" + sfx, [nq, TOK])
        dr["KT"] = d_int("KT" + sfx, [nk, TOK])
        dr["V"] = d_int("V" + sfx, [TOK, nv])
        dr["AT"] = d_int("AT" + sfx, [D, TOK])
        if kind == "na":
            dr["KH"] = d_int("KH" + sfx, [nk, 1024])
            dr["VH"] = d_int("VH" + sfx, [1024, nv])
            dr["GK"] = d_int("GK" + sfx, [1, 2, nk, 1024])
            dr["GV"] = d_int("GV" + sfx, [1, 2, 1024, nv])
        else:
            tpp = v_tpp(nv)
            dr["GK"] = d_int("GK" + sfx, [nk // KP_ROWS, 2, KP_ROWS, TOK])
            dr["GV"] = d_int("GV" + sfx, [TOK // tpp, 2, tpp, nv])
        drs.append(dr)
    with ExitStack() as es:
        kb = KB(nc, es)
        c = setup_consts(kb, es, ng)
        src = h_in
        skip = os.environ.get("KSKIP", "")
        for layer, dr in zip(layers, drs):
            if "A" not in skip:
                emit_A(kb, c, dr, layer, src)
            if "X" not in skip:
                emit_X(kb, c, dr, layer)
            if "B" not in skip:
                emit_B(kb, c, dr, layer, h)
            src = h
        kb.barrier()
    return nc, ins


_PROG_CACHE = {}


def _sw(w_in, kind, nqk):
    perm64 = _partner_perm(kind)
    cols = np.arange(nqk)
    perm = (cols // 64) * 64 + perm64[cols % 64]
    return np.ascontiguousarray(w_in[:, perm])


def kernel(x, norm_g, ffn1_wg, ffn1_wu, ffn1_wd, ffn2_wg, ffn2_wu, ffn2_wd,
           diff_w_in, diff_w_out, diff_lambda, diff_subln,
           na_w_in, na_w_out, na_rpb,
           gqa_w_in, gqa_w_out, gqa_qk_norm):
    f = lambda a: np.ascontiguousarray(np.asarray(a, dtype=np.float32))
    x = f(x); norm_g = f(norm_g)
    W = dict(ffn1_wg=f(ffn1_wg), ffn1_wu=f(ffn1_wu), ffn1_wd=f(ffn1_wd), ffn2_wg=f(ffn2_wg), ffn2_wu=f(ffn2_wu), ffn2_wd=f(ffn2_wd))
    w_in = {"diff": f(diff_w_in), "na": f(na_w_in), "gqa": f(gqa_w_in)}
    w_out = {"diff": f(diff_w_out), "na": f(na_w_out), "gqa": f(gqa_w_out)}
    diff_lambda = f(diff_lambda); diff_subln = f(diff_subln); na_rpb = f(na_rpb); gqa_qk_norm = f(gqa_qk_norm)
    ngl = np.ascontiguousarray(norm_g.reshape(4, 6, NKC, 128).transpose(3, 0, 1, 2))
    cores = list(range(NCORES))
    if "fused" not in _PROG_CACHE:
        _PROG_CACHE["fused"] = build_fused()
    nc, ins = _PROG_CACHE["fused"]

    shared = {"ng": ngl}
    per_half = [dict(), dict()]
    for layer in range(4):
        kind, j = MIX[layer], MIXJ[layer]
        sfx = "_L%d" % layer
        nq, nk, nv = mixer_dims(kind)
        shared["wg1" + sfx] = W["ffn1_wg"][layer]; shared["wu1" + sfx] = W["ffn1_wu"][layer]; shared["wd1" + sfx] = W["ffn1_wd"][layer]
        shared["wg2" + sfx] = W["ffn2_wg"][layer]; shared["wu2" + sfx] = W["ffn2_wu"][layer]; shared["wd2" + sfx] = W["ffn2_wd"][layer]
        shared["w_in" + sfx] = w_in[kind][j]
        shared["wo" + sfx] = w_out[kind][j]
        if kind != "na":
            shared["w_sw" + sfx] = _sw(w_in[kind][j], kind, nq + nk)
            for half in range(2):
                per_half[half]["tabs" + sfx] = rope_tables(kind, half)
        if kind == "gqa":
            p64 = _partner_perm("gqa")
            gq, gk = gqa_qk_norm[j, 0], gqa_qk_norm[j, 1]
            shared["gcols" + sfx] = np.ascontiguousarray(np.stack([np.tile(gq, 2), np.tile(gq[p64], 2), np.tile(gk, 2), np.tile(gk[p64], 2)], 1))
        if kind == "diff":
            shared["lam" + sfx] = np.ascontiguousarray(np.broadcast_to(diff_lambda[j].reshape(1, 256), (128, 256)))
            shared["subg" + sfx] = np.ascontiguousarray(diff_subln[j].reshape(128, 1))
        if kind == "na":
            for half in range(2):
                per_half[half]["natab" + sfx] = na_tables(na_rpb[j], half)
    in_maps = []
    for cc in cores:
        im = dict(shared)
        im.update(per_half[cc % 2])
        im["h_in"] = np.ascontiguousarray(x[cc // 2, (cc % 2) * TOK:(cc % 2 + 1) * TOK, :].T)
        in_maps.append({k: im[k] for k in ins})
    res = run_bass_kernel_spmd(nc, in_maps, core_ids=cores)
    out = np.empty((BATCH, SEQ, D), np.float32)
    for cc in cores:
        out[cc // 2, (cc % 2) * TOK:(cc % 2 + 1) * TOK, :] = np.asarray(res.results[cc]["h"]).T
    return out
```

```python
import numpy as np
import ml_dtypes
from contextlib import ExitStack
import concourse.bass as bass
import concourse.mybir as mybir
from concourse.bass_utils import run_bass_kernel_spmd

F32 = mybir.dt.float32
BF16 = mybir.dt.bfloat16
ALU = mybir.AluOpType
AF = mybir.ActivationFunctionType

D = 1024
DFF = 2816
import os
TOK = int(os.environ.get("KTOK", "4096"))
SEQ = 2 * TOK
BATCH = 4
NCORES = 8
TT = 256
EPS = 1e-6
NKC = D // 128
SAFE_SYNC = bool(os.environ.get("KSAFE"))
NFC = DFF // 128


class Tile:
    __slots__ = ("ap", "w", "r")

    def __init__(self, ap):
        self.ap = ap
        self.w = None
        self.r = {}


class KB:
    def __init__(self, nc, es):
        self.nc = nc
        self.es = es
        self.eng = {"pe": nc.tensor, "act": nc.scalar, "dve": nc.vector, "pool": nc.gpsimd, "sp": nc.sync}
        self.sems = {}
        self.cnt = {}
        for e in ("pe", "act", "dve", "pool"):
            self.sems[e] = es.enter_context(nc.semaphore("s_" + e))
            self.cnt[e] = 0
        self.nslots = 6
        self.dslot = {}
        for q in ("sp", "pool", "act"):
            self.dslot[q] = 0
            for s in range(self.nslots):
                k = ("d", q, s)
                self.sems[k] = es.enter_context(nc.semaphore("s_d%s%d" % (q, s)))
                self.cnt[k] = 0
        self.sems["cc"] = es.enter_context(nc.semaphore("s_cc"))
        self.cnt["cc"] = 0
        self.known = {e: {} for e in self.eng}
        self.pend_r = {e: [] for e in self.eng}
        self.pend_w = {e: [] for e in self.eng}
        self.uid = 0

    def sb(self, es, shape, dt, name=None):
        self.uid += 1
        return es.enter_context(self.nc.sbuf_tensor("%s_%d" % (name or "t", self.uid), list(shape), dt))

    def ps(self, es, shape, dt=F32, name=None):
        self.uid += 1
        return es.enter_context(self.nc.psum_tensor("%s_%d" % (name or "p", self.uid), list(shape), dt))

    def _wait(self, e, tok):
        k, v = tok
        if k == e and e == "pe":
            return
        if self.known[e].get(k, 0) >= v:
            return
        self.eng[e].wait_ge(self.sems[k], v)
        self.known[e][k] = v

    def _deps(self, reads, writes, e=None):
        deps = {}
        for t in reads:
            if t.w is not None:
                k, v = t.w
                deps[k] = max(deps.get(k, 0), v)
        for t in writes:
            if t.w is not None:
                k, v = t.w
                if k != e or SAFE_SYNC or self.cnt[e] - v < 2:
                    deps[k] = max(deps.get(k, 0), v)
            for k, v in t.r.items():
                if k != e or SAFE_SYNC or self.cnt[e] - v < 2:
                    deps[k] = max(deps.get(k, 0), v)
        return deps

    def _record(self, e, tok, reads, writes):
        k, v = tok
        for t in self.pend_r[e]:
            t.r[k] = v
        for t in self.pend_w[e]:
            t.w = tok
            t.r = {}
        self.pend_r[e] = []
        self.pend_w[e] = []
        for t in reads:
            t.r[k] = v
        for t in writes:
            t.w = tok
            t.r = {}

    def op(self, e, fn, reads=(), writes=(), signal=True):
        for k, v in self._deps(reads, writes, e).items():
            self._wait(e, (k, v))
        ins = fn(self.eng[e])
        if signal:
            self.cnt[e] += 1
            ins.then_inc(self.sems[e], 1)
            self._record(e, (e, self.cnt[e]), reads, writes)
        else:
            self.pend_r[e].extend(reads)
            self.pend_w[e].extend(writes)
        return ins

    def dma(self, q, out, in_, reads=(), writes=()):
        s = self.dslot[q]
        self.dslot[q] = (s + 1) % self.nslots
        k = ("d", q, s)
        deps = self._deps(reads, writes)
        if self.cnt[k] > 0:
            deps[k] = max(deps.get(k, 0), self.cnt[k])
        for kk, v in deps.items():
            self._wait(q, (kk, v))
        ins = self.eng[q].dma_start(out=out, in_=in_)
        self.cnt[k] += 16
        ins.then_inc(self.sems[k], 16)
        tok = (k, self.cnt[k])
        for t in reads:
            t.r[k] = self.cnt[k]
        for t in writes:
            t.w = tok
            t.r = {}
        return ins

    def allgather_pairs(self, src, dst):
        self.cnt["cc"] += 1
        self.nc.gpsimd.collective_compute("AllGather", ALU.bypass, replica_groups=[[0, 1], [2, 3], [4, 5], [6, 7]],
                                          ins=[src.opt()], outs=[dst.opt()]).then_inc(self.sems["cc"])

    def barrier(self):
        for e in self.eng:
            for k, v in self.cnt.items():
                if v > 0:
                    self._wait(e, (k, v))


class Ctx:
    pass


def setup_consts(kb, es, ng_dram):
    nc = kb.nc
    c = Ctx()
    c.ones_d = Tile(kb.sb(es, [128, 128], BF16, "ones_d")[:])
    c.ones_1 = Tile(kb.sb(es, [128, 128], BF16, "ones_1")[:])
    c.ones_128 = Tile(kb.sb(es, [128, 128], BF16, "ones_128")[:])
    c.blk64 = Tile(kb.sb(es, [128, 128], BF16, "blk64")[:])
    kb.op("dve", lambda e: e.memset(c.ones_d.ap, 1.0 / 1024.0), writes=[c.ones_d])
    kb.op("dve", lambda e: e.memset(c.ones_1.ap, 1.0), writes=[c.ones_1])
    kb.op("dve", lambda e: e.memset(c.ones_128.ap, 1.0 / 128.0), writes=[c.ones_128])
    kb.op("dve", lambda e: e.memset(c.blk64.ap, 0.0), writes=[c.blk64])
    kb.op("dve", lambda e: e.memset(c.blk64.ap[0:64, 0:64], 1.0 / 64.0), writes=[c.blk64])
    kb.op("dve", lambda e: e.memset(c.blk64.ap[64:128, 64:128], 1.0 / 64.0), writes=[c.blk64])
    c.ng = Tile(kb.sb(es, [128, 4, 6, NKC], F32, "ng")[:])
    kb.dma("sp", out=c.ng.ap, in_=ng_dram, writes=[c.ng])
    c.ngh = Tile(kb.sb(es, [128, 4, 6, NKC], F32, "ngh")[:])
    kb.op("dve", lambda e: e.tensor_scalar_mul(out=c.ngh.ap, in0=c.ng.ap, scalar1=0.5), reads=[c.ng], writes=[c.ngh])
    return c


def hview(hT, t, tt=TT):
    return hT.rearrange("(c p) t -> p c t", p=128)[:, :, t * tt:(t + 1) * tt]


def rstd_from_ps(kb, ps_ss, rstd, eps=EPS, tt=TT):
    kb.op("act", lambda e: e.activation(out=rstd.ap, in_=ps_ss.ap[:, 0:tt], func=AF.Sqrt, bias=eps, scale=1.0),
          reads=[ps_ss], writes=[rstd])
    kb.op("dve", lambda e: e.reciprocal(out=rstd.ap, in_=rstd.ap), reads=[rstd], writes=[rstd])


def prenorm(kb, c, h, sq, ps_ss, rstd, u_t, g_ap_fn):
    kb.op("act", lambda e: e.activation(out=sq.ap, in_=h.ap, func=AF.Square), reads=[h], writes=[sq])
    for k in range(NKC):
        kb.op("pe", lambda e, k=k: e.matmul(ps_ss.ap[:, 0:TT], lhsT=c.ones_d.ap, rhs=sq.ap[:, k, :], start=(k == 0), stop=(k == NKC - 1)),
              reads=[sq, c.ones_d], writes=[ps_ss], signal=(k == NKC - 1))
    rstd_from_ps(kb, ps_ss, rstd)
    for k in range(NKC):
        kb.op("dve", lambda e, k=k: e.scalar_tensor_tensor(out=u_t[k].ap, in0=h.ap[:, k, :], scalar=g_ap_fn(k), in1=rstd.ap,
                                                         op0=ALU.mult, op1=ALU.mult),
              reads=[h, rstd, c.ng, c.ngh], writes=[u_t[k]])


def postnorm_residual(kb, c, psy, h, sq_t, ps_ss, rstd, tmp_t, g_ap_fn, tt=TT):
    for i in range(NKC):
        kb.op("act", lambda e, i=i: e.activation(out=sq_t[i].ap, in_=psy[i][1], func=AF.Square),
              reads=[psy[i][0]], writes=[sq_t[i]])
    for i in range(NKC):
        kb.op("pe", lambda e, i=i: e.matmul(ps_ss.ap[:, 0:tt], lhsT=c.ones_d.ap, rhs=sq_t[i].ap, start=(i == 0), stop=(i == NKC - 1)),
              reads=[sq_t[i], c.ones_d], writes=[ps_ss], signal=(i == NKC - 1))
    rstd_from_ps(kb, ps_ss, rstd, tt=tt)
    for i in range(NKC):
        tmp = tmp_t[i % len(tmp_t)]
        kb.op("dve", lambda e, i=i, tmp=tmp: e.scalar_tensor_tensor(out=tmp.ap, in0=psy[i][1], scalar=g_ap_fn(i), in1=rstd.ap,
                                                                   op0=ALU.mult, op1=ALU.mult),
              reads=[psy[i][0], rstd, c.ng, c.ngh], writes=[tmp])
        kb.op("pool", lambda e, i=i, tmp=tmp: e.tensor_tensor(out=h.ap[:, i, :], in0=h.ap[:, i, :], in1=tmp.ap, op=ALU.add),
              reads=[tmp, h], writes=[h])


def load_cast_weight(kb, stg, src2d, dst_tiles, dst_ap_fn, nrow_chunks, ncol, state):
    half = 1408
    for k in range(nrow_chunks):
        for c0 in range(0, ncol, half):
            cw = min(half, ncol - c0)
            i = state[0]
            state[0] += 1
            st = stg[i % len(stg)]
            kb.dma("sp" if i % 2 == 0 else "pool", out=st.ap[:, :cw], in_=src2d[k * 128:(k + 1) * 128, c0:c0 + cw], writes=[st])
            ce = ("dve", "act")[i % 2]
            if ce == "act":
                kb.op("act", lambda e, st=st, k=k, c0=c0, cw=cw: e.copy(out=dst_ap_fn(k)[:, c0:c0 + cw], in_=st.ap[:, :cw]),
                      reads=[st], writes=[])
            else:
                kb.op("dve", lambda e, st=st, k=k, c0=c0, cw=cw: e.tensor_copy(out=dst_ap_fn(k)[:, c0:c0 + cw], in_=st.ap[:, :cw]),
                      reads=[st], writes=[])
            dst_tiles[k].append((ce, kb.cnt[ce]))


class WChunk:
    def __init__(self):
        self.toks = []


def wait_toks(kb, e, toks):
    for t in toks:
        kb._wait(e, t)


def ffn_phase(kb, c, hT_in, hT_out, wg, wu, wd, layer, jpre, jpost, ntiles=TOK // TT):
    nc = kb.nc
    with ExitStack() as es:
        wgb = kb.sb(es, [128, NKC, DFF], BF16, "wgb")
        wub = kb.sb(es, [128, NKC, DFF], BF16, "wub")
        wdb = kb.sb(es, [128, NFC, D], BF16, "wdb")
        stg = [Tile(kb.sb(es, [128, 1408], F32, "stg")[:]) for _ in range(3)]
        hb = [Tile(kb.sb(es, [128, NKC, TT], F32, "hb")[:]) for _ in range(2)]
        sq = Tile(kb.sb(es, [128, NKC, TT], BF16, "sq")[:])
        sq2_raw = kb.sb(es, [128, NKC, TT], BF16, "sq2")
        sq2_t = [Tile(sq2_raw[:, i, :]) for i in range(NKC)]
        u_raw = kb.sb(es, [128, NKC, TT], BF16, "u")
        u_t = [Tile(u_raw[:, k, :]) for k in range(NKC)]
        sg = [Tile(kb.sb(es, [128, TT], F32, "sg")[:]) for _ in range(2)]
        NACT = 8
        act = [Tile(kb.sb(es, [128, TT], BF16, "a")[:]) for _ in range(NACT)]
        tmp_t = [Tile(kb.sb(es, [128, TT], F32, "tmp")[:]) for _ in range(2)]
        rstd = Tile(kb.sb(es, [128, TT], F32, "rstd")[:])
        rstd2 = Tile(kb.sb(es, [128, TT], F32, "rstd2")[:])
        pgu = [Tile(kb.ps(es, [128, 512], F32, "pgu")[:]) for _ in range(2)]
        psyb = [Tile(kb.ps(es, [128, 512], F32, "py")[:]) for _ in range(4)]
        ps_ss = Tile(kb.ps(es, [128, 512], F32, "pss")[:])
        ps_ss2 = Tile(kb.ps(es, [128, 512], F32, "pss2")[:])
        psy = [(psyb[i // 2], psyb[i // 2].ap[:, (i % 2) * TT:(i % 2 + 1) * TT]) for i in range(NKC)]

        wg_tok = [[] for _ in range(NKC)]
        wu_tok = [[] for _ in range(NKC)]
        wd_tok = [[] for _ in range(NFC)]
        state = [0]
        load_cast_weight(kb, stg, wg, wg_tok, lambda k: wgb[:, k, :], NKC, DFF, state)
        load_cast_weight(kb, stg, wu, wu_tok, lambda k: wub[:, k, :], NKC, DFF, state)
        load_cast_weight(kb, stg, wd, wd_tok, lambda k: wdb[:, k, :], NFC, D, state)

        gpre = lambda k: c.ng.ap[:, layer, jpre, k:k + 1]
        gpost = lambda k: c.ngh.ap[:, layer, jpost, k:k + 1]
        u_raw2 = kb.sb(es, [128, NKC, TT], BF16, "u2")
        u_tt = [u_t, [Tile(u_raw2[:, k, :]) for k in range(NKC)]]

        def gu(j, u_c):
            p = pgu[j % 2]
            for k in range(NKC):
                kb.op("pe", lambda e, k=k: e.matmul(p.ap[:, 0:TT], lhsT=wgb[:, k, j * 128:(j + 1) * 128], rhs=u_c[k].ap,
                                                    start=(k == 0), stop=(k == NKC - 1)),
                      reads=[u_c[k]], writes=[p], signal=False)
            for k in range(NKC):
                kb.op("pe", lambda e, k=k: e.matmul(p.ap[:, TT:2 * TT], lhsT=wub[:, k, j * 128:(j + 1) * 128], rhs=u_c[k].ap,
                                                    start=(k == 0), stop=(k == NKC - 1)),
                      reads=[u_c[k]], writes=[p], signal=(k == NKC - 1))
            s_ = sg[j % 2]
            a = act[j % NACT]
            kb.op("act", lambda e: e.activation(out=s_.ap, in_=p.ap[:, 0:TT], func=AF.Silu), reads=[p], writes=[s_])
            kb.op("dve", lambda e: e.tensor_tensor(out=a.ap, in0=s_.ap, in1=p.ap[:, TT:2 * TT], op=ALU.mult), reads=[s_, p], writes=[a])

        def down(j):
            a = act[j % NACT]
            for i in range(NKC):
                kb.op("pe", lambda e, i=i: e.matmul(psy[i][1], lhsT=wdb[:, j, i * 128:(i + 1) * 128], rhs=a.ap,
                                                    start=(j == 0 and i % 2 == 0), stop=(j == NFC - 1),
                                                    skip_group_check=True),
                      reads=[a], writes=[psy[i][0]], signal=(i == NKC - 1))

        def tail(t):
            h = hb[t % 2]
            postnorm_residual(kb, c, psy, h, sq2_t, ps_ss2, rstd2, tmp_t, gpost)
            kb.dma("pool", out=hview(hT_out, t), in_=h.ap, reads=[h])

        kb.dma("sp", out=hb[0].ap, in_=hview(hT_in, 0), writes=[hb[0]])
        prenorm(kb, c, hb[0], sq, ps_ss, rstd, u_tt[0], gpre)
        if ntiles > 1:
            kb.dma("sp", out=hb[1].ap, in_=hview(hT_in, 1), writes=[hb[1]])
        for toks in wg_tok + wu_tok:
            wait_toks(kb, "pe", toks)
        LA = 6
        for t in range(ntiles):
            u_c = u_tt[t % 2]
            for j in range(LA):
                gu(j, u_c)
            if t > 0:
                tail(t - 1)
                if t + 1 < ntiles:
                    kb.dma("sp", out=hb[(t + 1) % 2].ap, in_=hview(hT_in, t + 1), writes=[hb[(t + 1) % 2]])
            else:
                for toks in wd_tok:
                    wait_toks(kb, "pe", toks)
            for j in range(LA, NFC):
                gu(j, u_c)
                down(j - LA)
                if j == 12 and t + 1 < ntiles:
                    prenorm(kb, c, hb[(t + 1) % 2], sq, ps_ss, rstd, u_tt[(t + 1) % 2], gpre)
            for j in range(NFC - LA, NFC):
                down(j)
        tail(ntiles - 1)
        kb.barrier()


def load_cast_simple(kb, stg, src2d, dst_fn, nrow_chunks, ncol, state, toks):
    half = 1408
    for k in range(nrow_chunks):
        for c0 in range(0, ncol, half):
            cw = min(half, ncol - c0)
            i = state[0]
            state[0] += 1
            st = stg[i % len(stg)]
            kb.dma("sp" if i % 2 == 0 else "pool", out=st.ap[:, :cw], in_=src2d[k * 128:(k + 1) * 128, c0:c0 + cw], writes=[st])
            ce = ("dve", "act")[i % 2]
            if ce == "act":
                kb.op("act", lambda e: e.copy(out=dst_fn(k, c0, cw), in_=st.ap[:, :cw]), reads=[st])
            else:
                kb.op("dve", lambda e: e.tensor_copy(out=dst_fn(k, c0, cw), in_=st.ap[:, :cw]), reads=[st])
            toks.append((ce, kb.cnt[ce]))


def qkv_phase(kb, c, hT_in, layer, fm, vspec, tabs, gcols, tok=TOK):
    ntiles = tok // TT
    with ExitStack() as es:
        nch = [f["ncols"] // 128 for f in fm]
        totch = sum(nch)
        wb = [kb.sb(es, [128, NKC, f["ncols"]], BF16, "wq") for f in fm]
        wsb = [kb.sb(es, [128, NKC, f["ncols"]], BF16, "wqs") if f["wsw"] is not None else None for f in fm]
        nv = vspec["nv"]
        wvb = kb.sb(es, [128, NKC, nv], BF16, "wv")
        stg = [Tile(kb.sb(es, [128, 1408], F32, "stg")[:]) for _ in range(3)]
        hb = [Tile(kb.sb(es, [128, NKC, TT], F32, "hb")[:]) for _ in range(2)]
        sq = Tile(kb.sb(es, [128, NKC, TT], BF16, "sq")[:])
        u_raw = kb.sb(es, [128, NKC, TT], BF16, "u")
        u_t = [Tile(u_raw[:, k, :]) for k in range(NKC)]
        rstd = Tile(kb.sb(es, [128, TT], F32, "rstd")[:])
        ntab = 0 if tabs is None else tabs.shape[0]
        tb = [Tile(kb.sb(es, [128, max(ntab, 1), TT], F32, "tab")[:]) for _ in range(2)]
        tbg = [Tile(kb.sb(es, [128, max(ntab, 1), TT], F32, "tabg")[:]) for _ in range(2)]
        gc = None
        if gcols is not None:
            gc = Tile(kb.sb(es, [128, gcols.shape[1]], F32, "gc")[:])
            kb.dma("sp", out=gc.ap, in_=gcols, writes=[gc])
        ostage = [[Tile(kb.sb(es, [128, n, TT], BF16, "ost")[:]) for n in nch] for _ in range(2)]
        vst = [Tile(kb.sb(es, [128, 512], BF16, "vst")[:]) for _ in range(2)]
        t1 = [Tile(kb.sb(es, [128, TT], F32, "t1")[:]) for _ in range(2)]
        t2 = [Tile(kb.sb(es, [128, TT], F32, "t2")[:]) for _ in range(2)]
        sqb = [Tile(kb.sb(es, [128, TT], BF16, "sqb")[:]) for _ in range(2)]
        rn = [Tile(kb.sb(es, [128, TT], F32, "rn")[:]) for _ in range(2)]
        ps_ss = Tile(kb.ps(es, [128, 512], F32, "pss")[:])
        pab = [Tile(kb.ps(es, [128, 512], F32, "pab")[:]) for _ in range(2)]
        pn = [Tile(kb.ps(es, [128, 512], F32, "pn")[:]) for _ in range(2)]
        pv = [Tile(kb.ps(es, [128, 512], F32, "pv")[:]) for _ in range(2)]

        toks = []
        state = [0]
        for fi, f in enumerate(fm):
            load_cast_simple(kb, stg, f["w"], lambda k, c0, cw, fi=fi: wb[fi][:, k, c0:c0 + cw], NKC, f["ncols"], state, toks)
            if f["wsw"] is not None:
                load_cast_simple(kb, stg, f["wsw"], lambda k, c0, cw, fi=fi: wsb[fi][:, k, c0:c0 + cw], NKC, f["ncols"], state, toks)
        load_cast_simple(kb, stg, vspec["w"], lambda k, c0, cw: wvb[:, k, c0:c0 + cw], NKC, nv, state, toks)
        wait_toks(kb, "pe", toks)

        gpre = lambda k: c.ng.ap[:, layer, 2, k:k + 1]
        kb.dma("sp", out=hb[0].ap, in_=hview(hT_in, 0), writes=[hb[0]])
        cnt = 0
        for t in range(ntiles):
            h = hb[t % 2]
            if t + 1 < ntiles:
                kb.dma("sp", out=hb[(t + 1) % 2].ap, in_=hview(hT_in, t + 1), writes=[hb[(t + 1) % 2]])
            tab = tb[t % 2]
            tabg = tbg[t % 2]
            if ntab:
                kb.dma("pool", out=tab.ap, in_=tabs.rearrange("n p t -> p n t")[:, :, t * TT:(t + 1) * TT], writes=[tab])
            prenorm(kb, c, h, sq, ps_ss, rstd, u_t, gpre)
            for fi, f in enumerate(fm):
                if f["mode"] == "normrope":
                    ci, si = f["tab"]
                    g0, g1 = f["g"]
                    kb.op("dve", lambda e: e.tensor_scalar_mul(out=tabg.ap[:, ci, :], in0=tab.ap[:, ci, :], scalar1=gc.ap[:, g0:g0 + 1]),
                          reads=[tab, gc], writes=[tabg])
                    kb.op("dve", lambda e: e.tensor_scalar_mul(out=tabg.ap[:, si, :], in0=tab.ap[:, si, :], scalar1=gc.ap[:, g1:g1 + 1]),
                          reads=[tab, gc], writes=[tabg])
            for fi, f in enumerate(fm):
                ost = ostage[t % 2][fi]
                for m in range(nch[fi]):
                    p = pab[cnt % 2]
                    x1 = t1[cnt % 2]
                    x2 = t2[cnt % 2]
                    has_sw = f["wsw"] is not None
                    for k in range(NKC):
                        kb.op("pe", lambda e, k=k: e.matmul(p.ap[:, 0:TT], lhsT=wb[fi][:, k, m * 128:(m + 1) * 128], rhs=u_t[k].ap,
                                                            start=(k == 0), stop=(k == NKC - 1)),
                              reads=[u_t[k]], writes=[p], signal=(k == NKC - 1 and not has_sw))
                    if has_sw:
                        for k in range(NKC):
                            kb.op("pe", lambda e, k=k: e.matmul(p.ap[:, TT:2 * TT], lhsT=wsb[fi][:, k, m * 128:(m + 1) * 128], rhs=u_t[k].ap,
                                                                start=(k == 0), stop=(k == NKC - 1)),
                                  reads=[u_t[k]], writes=[p], signal=(k == NKC - 1))
                    if f["mode"] == "plain":
                        sc = f.get("scale", 1.0)
                        kb.op("act", lambda e: e.activation(out=ost.ap[:, m, :], in_=p.ap[:, 0:TT], func=AF.Copy, scale=sc),
                              reads=[p], writes=[ost])
                    elif f["mode"] == "rope":
                        ci, si = f["tab"]
                        kb.op("dve", lambda e: e.tensor_tensor(out=x1.ap, in0=p.ap[:, 0:TT], in1=tab.ap[:, ci, :], op=ALU.mult),
                              reads=[p, tab], writes=[x1])
                        kb.op("dve", lambda e: e.tensor_tensor(out=x2.ap, in0=p.ap[:, TT:2 * TT], in1=tab.ap[:, si, :], op=ALU.mult),
                              reads=[p, tab], writes=[x2])
                        kb.op("pool", lambda e: e.tensor_tensor(out=ost.ap[:, m, :], in0=x1.ap, in1=x2.ap, op=ALU.add),
                              reads=[x1, x2], writes=[ost])
                    else:
                        ci, si = f["tab"]
                        sb_ = sqb[cnt % 2]
                        pnn = pn[cnt % 2]
                        r = rn[cnt % 2]
                        kb.op("act", lambda e: e.activation(out=sb_.ap, in_=p.ap[:, 0:TT], func=AF.Square), reads=[p], writes=[sb_])
                        kb.op("pe", lambda e: e.matmul(pnn.ap[:, 0:TT], lhsT=c.blk64.ap, rhs=sb_.ap, start=True, stop=True),
                              reads=[sb_, c.blk64], writes=[pnn])
                        rstd_from_ps(kb, pnn, r)
                        kb.op("dve", lambda e: e.tensor_tensor(out=x1.ap, in0=p.ap[:, 0:TT], in1=tabg.ap[:, ci, :], op=ALU.mult),
                              reads=[p, tabg], writes=[x1])
                        kb.op("dve", lambda e: e.tensor_tensor(out=x2.ap, in0=p.ap[:, TT:2 * TT], in1=tabg.ap[:, si, :], op=ALU.mult),
                              reads=[p, tabg], writes=[x2])
                        kb.op("pool", lambda e: e.tensor_tensor(out=x1.ap, in0=x1.ap, in1=x2.ap, op=ALU.add),
                              reads=[x1, x2], writes=[x1])
                        kb.op("pool", lambda e: e.tensor_tensor(out=ost.ap[:, m, :], in0=x1.ap, in1=r.ap, op=ALU.mult),
                              reads=[x1, r], writes=[ost])
                    cnt += 1
                kb.dma("sp", out=f["out"].rearrange("(m p) t -> p m t", p=128)[:, :, t * TT:(t + 1) * TT], in_=ost.ap, reads=[ost])
            vi = 0
            for ts in range(TT // 128):
                for cb in range(0, nv, 512):
                    cw = min(512, nv - cb)
                    p = pv[vi % 2]
                    vs = vst[vi % 2]
                    vi += 1
                    for k in range(NKC):
                        kb.op("pe", lambda e, k=k: e.matmul(p.ap[:, 0:cw], lhsT=u_t[k].ap[:, ts * 128:(ts + 1) * 128], rhs=wvb[:, k, cb:cb + cw],
                                                            start=(k == 0), stop=(k == NKC - 1)),
                              reads=[u_t[k]], writes=[p], signal=(k == NKC - 1))
                    kb.op("act", lambda e: e.copy(out=vs.ap[:, 0:cw], in_=p.ap[:, 0:cw]), reads=[p], writes=[vs])
                    kb.dma("pool", out=vspec["out"][t * TT + ts * 128:t * TT + (ts + 1) * 128, cb:cb + cw], in_=vs.ap[:, 0:cw], reads=[vs])
        kb.barrier()


QT_ = 512


def attn_phase_half(kb, c, QT, AT, spec, tok=TOK, skeys=SEQ):
    kind = spec["kind"]
    dv = spec["dv"]
    n_out = spec["n_out"]
    comps = spec["comps"]
    nqt = tok // QT_
    nkc_all = skeys // 128
    with ExitStack() as es:
        NB = 2
        ktb = [[Tile(kb.sb(es, [64, skeys], BF16, "kt")[:]) for _ in range(comps)] for _ in range(NB)]
        qtb = [[Tile(kb.sb(es, [64, tok], BF16, "qt")[:]) for _ in range(comps)] for _ in range(NB)]
        vb = [Tile(kb.sb(es, [128, nkc_all, dv], BF16, "v")[:]) for _ in range(NB)]
        pbuf = [Tile(kb.sb(es, [128, QT_], BF16, "p")[:]) for _ in range(4)]
        rs = [Tile(kb.sb(es, [128, QT_], F32, "rs")[:]) for _ in range(2)]
        on = [[Tile(kb.sb(es, [128, QT_], F32, "on")[:]) for _ in range(comps)] for _ in range(2)]
        ost = [Tile(kb.sb(es, [128, QT_], BF16, "ost")[:]) for _ in range(2)]
        psS = [Tile(kb.ps(es, [128, 512], F32, "pS")[:]) for _ in range(3)]
        psO = [Tile(kb.ps(es, [128, 512], F32, "pO")[:]) for _ in range(2)]
        psM = [Tile(kb.ps(es, [128, 512], F32, "pM")[:]) for _ in range(2)]
        if kind == "diff":
            psN = Tile(kb.ps(es, [128, 512], F32, "pN")[:])
            od = [Tile(kb.sb(es, [128, QT_], F32, "od")[:]) for _ in range(2)]
            sqd = [Tile(kb.sb(es, [128, QT_], BF16, "sqd")[:]) for _ in range(2)]
            rd = [Tile(kb.sb(es, [128, QT_], F32, "rd")[:]) for _ in range(2)]
            lv = Tile(kb.sb(es, [128, 256], F32, "lv")[:])
            lt = Tile(kb.sb(es, [128, 64], F32, "lt")[:])
            ls = Tile(kb.sb(es, [128, 4], F32, "ls")[:])
            sgc = Tile(kb.sb(es, [128, 2], F32, "sgc")[:])
            kb.dma("sp", out=lv.ap, in_=spec["lam"], writes=[lv])
            kb.dma("sp", out=sgc.ap[:, 0:1], in_=spec["subg"], writes=[sgc])
            for i in range(2):
                kb.op("dve", lambda e: e.tensor_tensor(out=lt.ap, in0=lv.ap[:, (2 * i) * 64:(2 * i + 1) * 64],
                                                       in1=lv.ap[:, (2 * i + 1) * 64:(2 * i + 2) * 64], op=ALU.mult), reads=[lv], writes=[lt])
                kb.op("dve", lambda e: e.reduce_sum(out=ls.ap[:, i:i + 1], in_=lt.ap, axis=mybir.AxisListType.X), reads=[lt], writes=[ls])
            kb.op("act", lambda e: e.activation(out=ls.ap[:, 0:2], in_=ls.ap[:, 0:2], func=AF.Exp), reads=[ls], writes=[ls])
            kb.op("dve", lambda e: e.tensor_tensor(out=ls.ap[:, 2:3], in0=ls.ap[:, 1:2], in1=ls.ap[:, 0:1], op=ALU.subtract), reads=[ls], writes=[ls])
            kb.op("dve", lambda e: e.tensor_scalar_add(out=ls.ap[:, 2:3], in0=ls.ap[:, 2:3], scalar1=-spec["lam_init"]), reads=[ls], writes=[ls])
            kb.op("dve", lambda e: e.tensor_scalar_mul(out=sgc.ap[:, 1:2], in0=sgc.ap[:, 0:1], scalar1=1.0 - spec["lam_init"]), reads=[sgc], writes=[sgc])
        if kind == "na":
            ident = Tile(kb.sb(es, [128, 128], BF16, "ident")[:])
            identf = Tile(kb.sb(es, [128, 128], F32, "identf")[:])
            kb.op("pool", lambda e: e.memset(identf.ap, 0.0), writes=[identf])
            kb.op("pool", lambda e: e.affine_select(out=identf.ap, in_=identf.ap, pattern=[[-1, 128]], compare_op=ALU.not_equal,
                                                    fill=1.0, base=0, channel_multiplier=1), reads=[identf], writes=[identf])
            kb.op("pool", lambda e: e.tensor_copy(out=ident.ap, in_=identf.ap), reads=[identf], writes=[ident])
            tabf = [Tile(kb.sb(es, [128, 8, 512], F32, "tabf")[:]) for _ in range(3)]
            tabb = [[Tile(kb.sb(es, [128, 8, 512], BF16, "tabb")[:]) for _ in range(3)] for _ in range(NB)]

        def load_head(oh, slot):
            qi = 0
            for cp in range(comps):
                qr = spec["qrow"](oh, cp)
                for (c0, ncol, src) in spec["kload"](oh, cp):
                    kb.dma(("sp", "pool")[qi % 2], out=ktb[slot][cp].ap[:, c0:c0 + ncol], in_=src, writes=[ktb[slot][cp]])
                    qi += 1
                kb.dma("pool", out=qtb[slot][cp].ap, in_=QT[qr * 64:(qr + 1) * 64, :], writes=[qtb[slot][cp]])
            for (k0, nk_, src) in spec["vload"](oh):
                kb.dma(("sp", "pool")[qi % 2], out=vb[slot].ap[:, k0:k0 + nk_, :], in_=src, writes=[vb[slot]])
                qi += 1
            if kind == "na":
                for cl in range(3):
                    kb.dma("sp", out=tabf[cl].ap, in_=spec["tab"][cl, oh], writes=[tabf[cl]])
                    kb.op("pool", lambda e: e.tensor_copy(out=tabb[slot][cl].ap, in_=tabf[cl].ap), reads=[tabf[cl]], writes=[tabb[slot][cl]])

        steps = []
        for oh in range(n_out):
            for qt in range(nqt):
                st, n = spec["band"](qt)
                for cp in range(comps):
                    for m in range(n):
                        steps.append((oh, qt, cp, st + m, m, m == 0, m == n - 1))

        def qk(n):
            oh, qt, cp, kc, m, first, last = steps[n]
            slot = oh % NB
            p = psS[n % 3]
            has_b = kind == "na"
            kb.op("pe", lambda e: e.matmul(p.ap, lhsT=ktb[slot][cp].ap[:, kc * 128:(kc + 1) * 128], rhs=qtb[slot][cp].ap[:, qt * QT_:(qt + 1) * QT_],
                                           start=True, stop=not has_b),
                  reads=[ktb[slot][cp], qtb[slot][cp]], writes=[p], signal=not has_b)
            if has_b:
                tb_ = tabb[slot][spec["cls"](qt)]
                kb.op("pe", lambda e: e.matmul(p.ap, lhsT=ident.ap, rhs=tb_.ap[:, m, :], start=False, stop=True),
                      reads=[tb_, ident], writes=[p])

        fin_i = [0]

        def pvstep(n):
            oh, qt, cp, kc, m, first, last = steps[n]
            slot = oh % NB
            p = psS[n % 3]
            pb = pbuf[n % 4]
            grp = (oh * nqt + qt) * comps + cp
            po = psO[grp % 2]
            pm = psM[grp % 2]
            kb.op("act", lambda e: e.activation(out=pb.ap, in_=p.ap, func=AF.Exp), reads=[p], writes=[pb])
            kb.op("pe", lambda e: e.matmul(po.ap[0:dv, :], lhsT=vb[slot].ap[:, kc, :], rhs=pb.ap, start=first, stop=last),
                  reads=[vb[slot], pb], writes=[po], signal=last)
            kb.op("pe", lambda e: e.matmul(pm.ap[0:dv, :], lhsT=c.ones_1.ap[:, 0:dv], rhs=pb.ap, start=first, stop=last),
                  reads=[c.ones_1, pb], writes=[pm], signal=True)
            if not last:
                return
            g2 = (oh * nqt + qt) % 2
            r = rs[grp % 2]
            o_n = on[g2][cp]
            kb.op("dve", lambda e: e.reciprocal(out=r.ap[0:dv, :], in_=pm.ap[0:dv, :]), reads=[pm], writes=[r])
            if kind != "diff":
                os_ = ost[g2]
                kb.op("dve", lambda e: e.tensor_tensor(out=os_.ap[0:dv, :], in0=po.ap[0:dv, :], in1=r.ap[0:dv, :], op=ALU.mult),
                      reads=[po, r], writes=[os_])
                kb.dma("pool", out=AT[oh * dv:(oh + 1) * dv, qt * QT_:(qt + 1) * QT_], in_=os_.ap[0:dv, :], reads=[os_])
                return
            kb.op("dve", lambda e: e.tensor_tensor(out=o_n.ap, in0=po.ap, in1=r.ap, op=ALU.mult), reads=[po, r], writes=[o_n])
            if cp != comps - 1:
                return
            o = od[g2]
            s_ = sqd[g2]
            rr = rd[g2]
            os_ = ost[g2]
            kb.op("dve", lambda e: e.scalar_tensor_tensor(out=o.ap, in0=on[g2][1].ap, scalar=ls.ap[:, 2:3], in1=on[g2][0].ap,
                                                          op0=ALU.mult, op1=ALU.add), reads=[on[g2][0], on[g2][1], ls], writes=[o])
            kb.op("act", lambda e: e.activation(out=s_.ap, in_=o.ap, func=AF.Square), reads=[o], writes=[s_])
            kb.op("pe", lambda e: e.matmul(psN.ap, lhsT=c.ones_128.ap, rhs=s_.ap, start=True, stop=True), reads=[s_, c.ones_128], writes=[psN])
            rstd_from_ps(kb, psN, rr, tt=QT_)
            kb.op("dve", lambda e: e.scalar_tensor_tensor(out=os_.ap, in0=o.ap, scalar=sgc.ap[:, 1:2], in1=rr.ap, op0=ALU.mult, op1=ALU.mult),
                  reads=[o, rr, sgc], writes=[os_])
            kb.dma("pool", out=AT[oh * 128:(oh + 1) * 128, qt * QT_:(qt + 1) * QT_], in_=os_.ap, reads=[os_])

        load_head(0, 0)
        N = len(steps)
        cur = -1
        for n in range(N):
            oh = steps[n][0]
            if oh != cur:
                cur = oh
                if oh + 1 < n_out:
                    load_head(oh + 1, (oh + 1) % NB)
            if n == 0:
                qk(0)
                if N > 1:
                    qk(1)
            if n + 2 < N:
                qk(n + 2)
            pvstep(n)
        kb.barrier()


def attn_phase(kb, c, QT, AT, spec, tok=TOK, skeys=SEQ):
    kind = spec["kind"]
    if kind == "diff" and os.environ.get("KDIFF_HALF"):
        return attn_phase_half(kb, c, QT, AT, spec, tok=tok, skeys=skeys)
    dv = spec["dv"]
    n_out = spec["n_out"]
    comps = spec["comps"]
    nqt = tok // QT_
    nkc_all = skeys // 128
    with ExitStack() as es:
        NB = 2
        ktb = [Tile(kb.sb(es, [128, skeys], BF16, "kt")[:]) for _ in range(NB)]
        qtb = [[Tile(kb.sb(es, [128, tok], BF16, "qt")[:]) for _ in range(comps)] for _ in range(NB)]
        vb = [Tile(kb.sb(es, [128, nkc_all, 128], BF16, "v")[:]) for _ in range(NB)]
        if dv == 64:
            for b in range(NB):
                kb.op("pool", lambda e: e.memset(vb[b].ap[:, :, 64:128], 1.0), writes=[vb[b]])
        pbuf = [Tile(kb.sb(es, [128, QT_], BF16, "p")[:]) for _ in range(4)]
        rs = [Tile(kb.sb(es, [128, QT_], F32, "rs")[:]) for _ in range(2)]
        on = [[Tile(kb.sb(es, [128, QT_], F32, "on")[:]) for _ in range(comps)] for _ in range(2)]
        ost = [Tile(kb.sb(es, [128, QT_], BF16, "ost")[:]) for _ in range(2)]
        psS = [Tile(kb.ps(es, [128, 512], F32, "pS")[:]) for _ in range(3)]
        psO = [Tile(kb.ps(es, [128, 512], F32, "pO")[:]) for _ in range(2)]
        psM = [Tile(kb.ps(es, [128, 512], F32, "pM")[:]) for _ in range(2)]
        if kind == "diff":
            psN = Tile(kb.ps(es, [128, 512], F32, "pN")[:])
            od = [Tile(kb.sb(es, [128, QT_], F32, "od")[:]) for _ in range(2)]
            sqd = [Tile(kb.sb(es, [128, QT_], BF16, "sqd")[:]) for _ in range(2)]
            rd = [Tile(kb.sb(es, [128, QT_], F32, "rd")[:]) for _ in range(2)]
            lv = Tile(kb.sb(es, [128, 256], F32, "lv")[:])
            lt = Tile(kb.sb(es, [128, 64], F32, "lt")[:])
            ls = Tile(kb.sb(es, [128, 4], F32, "ls")[:])
            sgc = Tile(kb.sb(es, [128, 2], F32, "sgc")[:])
            kb.dma("sp", out=lv.ap, in_=spec["lam"], writes=[lv])
            kb.dma("sp", out=sgc.ap[:, 0:1], in_=spec["subg"], writes=[sgc])
            for i in range(2):
                kb.op("dve", lambda e: e.tensor_tensor(out=lt.ap, in0=lv.ap[:, (2 * i) * 64:(2 * i + 1) * 64],
                                                       in1=lv.ap[:, (2 * i + 1) * 64:(2 * i + 2) * 64], op=ALU.mult), reads=[lv], writes=[lt])
                kb.op("dve", lambda e: e.reduce_sum(out=ls.ap[:, i:i + 1], in_=lt.ap, axis=mybir.AxisListType.X), reads=[lt], writes=[ls])
            kb.op("act", lambda e: e.activation(out=ls.ap[:, 0:2], in_=ls.ap[:, 0:2], func=AF.Exp), reads=[ls], writes=[ls])
            kb.op("dve", lambda e: e.tensor_tensor(out=ls.ap[:, 2:3], in0=ls.ap[:, 1:2], in1=ls.ap[:, 0:1], op=ALU.subtract), reads=[ls], writes=[ls])
            kb.op("dve", lambda e: e.tensor_scalar_add(out=ls.ap[:, 2:3], in0=ls.ap[:, 2:3], scalar1=-spec["lam_init"]), reads=[ls], writes=[ls])
            kb.op("dve", lambda e: e.tensor_scalar_mul(out=sgc.ap[:, 1:2], in0=sgc.ap[:, 0:1], scalar1=1.0 - spec["lam_init"]), reads=[sgc], writes=[sgc])
        if kind == "na":
            ident = Tile(kb.sb(es, [128, 128], BF16, "ident")[:])
            identf = Tile(kb.sb(es, [128, 128], F32, "identf")[:])
            kb.op("pool", lambda e: e.memset(identf.ap, 0.0), writes=[identf])
            kb.op("pool", lambda e: e.affine_select(out=identf.ap, in_=identf.ap, pattern=[[-1, 128]], compare_op=ALU.not_equal,
                                                    fill=1.0, base=0, channel_multiplier=1), reads=[identf], writes=[identf])
            kb.op("pool", lambda e: e.tensor_copy(out=ident.ap, in_=identf.ap), reads=[identf], writes=[ident])
            tabf = [Tile(kb.sb(es, [128, 8, 512], F32, "tabf")[:]) for _ in range(3)]
            tabb = [[Tile(kb.sb(es, [128, 8, 512], BF16, "tabb")[:]) for _ in range(3)] for _ in range(NB)]

        def load_head(oh, slot):
            qi = 0
            for (c0, ncol, src) in spec["kload"](oh):
                kb.dma(("sp", "pool")[qi % 2], out=ktb[slot].ap[:, c0:c0 + ncol], in_=src, writes=[ktb[slot]])
                qi += 1
            for cp in range(comps):
                qr = spec["qrow"](oh, cp)
                hs = spec["khalf"](oh, cp)
                kb.op("pool", lambda e: e.memset(qtb[slot][cp].ap[(1 - hs) * 64:(2 - hs) * 64, :], 0.0), writes=[qtb[slot][cp]])
                kb.dma("pool", out=qtb[slot][cp].ap[hs * 64:(hs + 1) * 64, :], in_=QT[qr * 64:(qr + 1) * 64, :], writes=[qtb[slot][cp]])
            for (k0, nk_, src) in spec["vload"](oh):
                kb.dma(("sp", "pool")[qi % 2], out=vb[slot].ap[:, k0:k0 + nk_, 0:dv], in_=src, writes=[vb[slot]])
                qi += 1
            if kind == "na":
                for cl in range(3):
                    kb.dma("sp", out=tabf[cl].ap, in_=spec["tab"][cl, oh], writes=[tabf[cl]])
                    kb.op("pool", lambda e: e.tensor_copy(out=tabb[slot][cl].ap, in_=tabf[cl].ap), reads=[tabf[cl]], writes=[tabb[slot][cl]])

        steps = []
        for oh in range(n_out):
            for qt in range(nqt):
                st, n = spec["band"](qt)
                for cp in range(comps):
                    for m in range(n):
                        steps.append((oh, qt, cp, st + m, m, m == 0, m == n - 1))

        def qk(n):
            oh, qt, cp, kc, m, first, last = steps[n]
            slot = oh % NB
            p = psS[n % 3]
            has_b = kind == "na"
            kb.op("pe", lambda e: e.matmul(p.ap, lhsT=ktb[slot].ap[:, kc * 128:(kc + 1) * 128], rhs=qtb[slot][cp].ap[:, qt * QT_:(qt + 1) * QT_],
                                           start=True, stop=not has_b),
                  reads=[ktb[slot], qtb[slot][cp]], writes=[p], signal=not has_b)
            if has_b:
                tb_ = tabb[slot][spec["cls"](qt)]
                kb.op("pe", lambda e: e.matmul(p.ap, lhsT=ident.ap, rhs=tb_.ap[:, m, :], start=False, stop=True),
                      reads=[tb_, ident], writes=[p])

        fin_i = [0]

        def pvstep(n):
            oh, qt, cp, kc, m, first, last = steps[n]
            slot = oh % NB
            p = psS[n % 3]
            pb = pbuf[n % 4]
            grp = (oh * nqt + qt) * comps + cp
            po = psO[grp % 2]
            pm = psM[grp % 2]
            kb.op("act", lambda e: e.activation(out=pb.ap, in_=p.ap, func=AF.Exp), reads=[p], writes=[pb])
            if dv == 64:
                kb.op("pe", lambda e: e.matmul(po.ap, lhsT=vb[slot].ap[:, kc, :], rhs=pb.ap, start=first, stop=last),
                      reads=[vb[slot], pb], writes=[po], signal=True)
            else:
                kb.op("pe", lambda e: e.matmul(po.ap, lhsT=vb[slot].ap[:, kc, :], rhs=pb.ap, start=first, stop=last),
                      reads=[vb[slot], pb], writes=[po], signal=last)
                kb.op("pe", lambda e: e.matmul(pm.ap, lhsT=c.ones_1.ap, rhs=pb.ap, start=first, stop=last),
                      reads=[c.ones_1, pb], writes=[pm], signal=True)
            if not last:
                return
            g2 = (oh * nqt + qt) % 2
            r = rs[grp % 2]
            o_n = on[g2][cp]
            if kind != "diff":
                os_ = ost[g2]
                kb.op("dve", lambda e: e.reciprocal(out=r.ap[64:128, :], in_=po.ap[64:128, :]), reads=[po], writes=[r])
                kb.op("dve", lambda e: e.tensor_tensor(out=os_.ap[0:64, :], in0=po.ap[0:64, :], in1=r.ap[64:128, :], op=ALU.mult),
                      reads=[po, r], writes=[os_])
                kb.dma("pool", out=AT[oh * dv:(oh + 1) * dv, qt * QT_:(qt + 1) * QT_], in_=os_.ap[0:dv, :], reads=[os_])
                return
            kb.op("dve", lambda e: e.reciprocal(out=r.ap, in_=pm.ap), reads=[pm], writes=[r])
            kb.op("dve", lambda e: e.tensor_tensor(out=o_n.ap, in0=po.ap, in1=r.ap, op=ALU.mult), reads=[po, r], writes=[o_n])
            if cp != comps - 1:
                return
            o = od[g2]
            s_ = sqd[g2]
            rr = rd[g2]
            os_ = ost[g2]
            kb.op("dve", lambda e: e.scalar_tensor_tensor(out=o.ap, in0=on[g2][1].ap, scalar=ls.ap[:, 2:3], in1=on[g2][0].ap,
                                                          op0=ALU.mult, op1=ALU.add), reads=[on[g2][0], on[g2][1], ls], writes=[o])
            kb.op("act", lambda e: e.activation(out=s_.ap, in_=o.ap, func=AF.Square), reads=[o], writes=[s_])
            kb.op("pe", lambda e: e.matmul(psN.ap, lhsT=c.ones_128.ap, rhs=s_.ap, start=True, stop=True), reads=[s_, c.ones_128], writes=[psN])
            rstd_from_ps(kb, psN, rr, tt=QT_)
            kb.op("dve", lambda e: e.scalar_tensor_tensor(out=os_.ap, in0=o.ap, scalar=sgc.ap[:, 1:2], in1=rr.ap, op0=ALU.mult, op1=ALU.mult),
                  reads=[o, rr, sgc], writes=[os_])
            kb.dma("pool", out=AT[oh * 128:(oh + 1) * 128, qt * QT_:(qt + 1) * QT_], in_=os_.ap, reads=[os_])

        load_head(0, 0)
        N = len(steps)
        cur = -1
        for n in range(N):
            oh = steps[n][0]
            if oh != cur:
                cur = oh
                if oh + 1 < n_out:
                    load_head(oh + 1, (oh + 1) % NB)
            if n == 0:
                qk(0)
                if N > 1:
                    qk(1)
            if n + 2 < N:
                qk(n + 2)
            pvstep(n)
        kb.barrier()


def outproj_phase(kb, c, hT_in, hT_out, AT, wo, layer, tok=TOK):
    ntiles = tok // TT
    with ExitStack() as es:
        wob = kb.sb(es, [128, NKC, D], BF16, "wo")
        stg = [Tile(kb.sb(es, [128, 1408], F32, "stg")[:]) for _ in range(3)]
        hb = [Tile(kb.sb(es, [128, NKC, TT], F32, "hb")[:]) for _ in range(2)]
        ab = [Tile(kb.sb(es, [128, NKC, TT], BF16, "ab")[:]) for _ in range(2)]
        sq2_raw = kb.sb(es, [128, NKC, TT], BF16, "sq2")
        sq2_t = [Tile(sq2_raw[:, i, :]) for i in range(NKC)]
        tmp_t = [Tile(kb.sb(es, [128, TT], F32, "tmp")[:]) for _ in range(2)]
        rstd2 = Tile(kb.sb(es, [128, TT], F32, "rstd2")[:])
        psyb = [Tile(kb.ps(es, [128, 512], F32, "py")[:]) for _ in range(4)]
        ps_ss2 = Tile(kb.ps(es, [128, 512], F32, "pss2")[:])
        psy = [(psyb[i // 2], psyb[i // 2].ap[:, (i % 2) * TT:(i % 2 + 1) * TT]) for i in range(NKC)]
        toks = []
        load_cast_simple(kb, stg, wo, lambda k, c0, cw: wob[:, k, c0:c0 + cw], NKC, D, [0], toks)
        wait_toks(kb, "pe", toks)
        gpost = lambda k: c.ng.ap[:, layer, 3, k:k + 1]
        for t in range(ntiles):
            h = hb[t % 2]
            a = ab[t % 2]
            kb.dma("sp", out=h.ap, in_=hview(hT_in, t), writes=[h])
            kb.dma("pool", out=a.ap, in_=hview(AT, t), writes=[a])
            for k in range(NKC):
                for i in range(NKC):
                    kb.op("pe", lambda e: e.matmul(psy[i][1], lhsT=wob[:, k, i * 128:(i + 1) * 128], rhs=a.ap[:, k, :],
                                                   start=(k == 0 and i % 2 == 0), stop=(k == NKC - 1), skip_group_check=True),
                          reads=[a], writes=[psy[i][0]], signal=(k == NKC - 1))
            postnorm_residual(kb, c, psy, h, sq2_t, ps_ss2, rstd2, tmp_t, gpost)
            kb.dma("pool", out=hview(hT_out, t), in_=h.ap, reads=[h])
        kb.barrier()


MIX = ["diff", "na", "gqa", "diff"]
MIXJ = [0, 0, 0, 1]
BF = ml_dtypes.bfloat16


def _partner_perm(kind):
    d = np.arange(64)
    if kind == "diff":
        return np.where(d < 8, d + 8, np.where(d < 16, d - 8, d))
    return np.where(d % 32 < 16, d + 16, d - 16)


def rope_tables(kind, half, tok=TOK):
    pos = (half * tok + np.arange(tok)).astype(np.float32)
    C = np.ones((64, tok), np.float32)
    S = np.zeros((64, tok), np.float32)
    if kind == "diff":
        inv = (np.float32(500000.0) ** (-(np.arange(0, 16, 2, dtype=np.float32)) / np.float32(16))).astype(np.float32)
        ang = (pos[None, :] * inv[:, None]).astype(np.float32)
        C[0:8] = np.cos(ang); C[8:16] = np.cos(ang)
        S[0:8] = -np.sin(ang); S[8:16] = np.sin(ang)
    else:
        inv = (np.float32(10000.0) ** (-(np.arange(0, 32, 2, dtype=np.float32)) / np.float32(32))).astype(np.float32)
        ip = (half * tok + np.arange(tok))
        ar = ((ip // 64).astype(np.float32)[None, :] * inv[:, None]).astype(np.float32)
        ac = ((ip % 64).astype(np.float32)[None, :] * inv[:, None]).astype(np.float32)
        C[0:16] = np.cos(ar); C[16:32] = np.cos(ar); C[32:48] = np.cos(ac); C[48:64] = np.cos(ac)
        S[0:16] = -np.sin(ar); S[16:32] = np.sin(ar); S[32:48] = -np.sin(ac); S[48:64] = np.sin(ac)
    C2 = np.concatenate([C, C], 0)
    S2 = np.concatenate([S, S], 0)
    sc = np.float32(0.125)
    return np.ascontiguousarray(np.stack([C2 * sc, S2 * sc, C2, S2]).astype(np.float32))


def na_tables(rpb, half):
    out = np.empty((3, 16, 128, 8, 512), np.float32)
    p = np.arange(128)
    q = np.arange(512)
    R0s = [0, 56, 8] if half == 0 else [64, 120, 72]
    for cls in range(3):
        R0 = R0s[cls]
        r = R0 + q // 64
        cq = q % 64
        r0 = np.clip(r - 4, 0, 120)
        c0 = np.clip(cq - 8, 0, 48)
        for m in range(8):
            kr = R0 - 4 + 2 * m + p // 64
            kc = p % 64
            inwin = ((kr[:, None] >= r0[None, :]) & (kr[:, None] < r0[None, :] + 8) &
                     (kc[:, None] >= c0[None, :]) & (kc[:, None] < c0[None, :] + 16))
            dr = np.clip(kr[:, None] - r[None, :] + 7, 0, 14)
            dc = np.clip(kc[:, None] - cq[None, :] + 15, 0, 30)
            g = rpb[:, dr, dc]
            out[cls, :, :, m, :] = np.where(inwin[None], g, np.float32(-30000.0))
    return out


NA_ROWS = 80
NA_KEYS = NA_ROWS * 64


def na_band(qt):
    return 4 * qt, 8


def na_cls(qt):
    return 0 if qt == 0 else (1 if qt == TOK // QT_ - 1 else 2)


def mixer_dims(kind):
    if kind == "gqa":
        return 1024, 256, 256
    return 1024, 1024, 1024


KP_ROWS = 256


def v_tpp(nv):
    return (2 * 1024 * 1024) // (nv * 2)


def vview(ap2d, c0, dv):
    return ap2d.rearrange("(kc p) e -> p kc e", p=128)[:, :, c0:c0 + dv]


def attn_spec(kind, layer, dr):
    GK, GV, KT, V = dr["GK"], dr["GV"], dr["KT"], dr["V"]
    nq, nk, nv = mixer_dims(kind)
    tpp = v_tpp(nv)

    def kload_full(pair):
        piece, r0 = (pair * 128) // KP_ROWS, (pair * 128) % KP_ROWS
        return [(rank * TOK, TOK, GK[piece, rank, r0:r0 + 128, :]) for rank in range(2)]

    def vload_full(vc, dv):
        res = []
        for rank in range(2):
            for piece in range(TOK // tpp):
                res.append(((rank * TOK + piece * tpp) // 128, tpp // 128, vview(GV[piece, rank], vc, dv)))
        return res

    if kind == "diff":
        def kload64(kr):
            piece, r0 = (kr * 64) // KP_ROWS, (kr * 64) % KP_ROWS
            return [(rank * TOK, TOK, GK[piece, rank, r0:r0 + 64, :]) for rank in range(2)]
        return dict(kind="diff", n_out=8, comps=2, dv=128, qrow=lambda oh, cp: oh * 2 + cp, khalf=lambda oh, cp: cp,
                    kload=(lambda oh: kload_full(oh)) if not os.environ.get("KDIFF_HALF") else (lambda oh, cp: kload64(oh * 2 + cp)),
                    vload=lambda oh: vload_full(oh * 128, 128),
                    band=lambda qt: (0, SEQ // 128), lam=dr["lam"], subg=dr["subg"],
                    lam_init=0.8 - 0.6 * float(np.exp(-0.3 * layer)))
    if kind == "gqa":
        return dict(kind="gqa", n_out=16, comps=1, dv=64, qrow=lambda oh, cp: oh, khalf=lambda oh, cp: (oh // 4) % 2,
                    kload=lambda oh: kload_full(oh // 8), vload=lambda oh: vload_full((oh // 4) * 64, 64),
                    band=lambda qt: (0, SEQ // 128))
    return dict(kind="na", n_out=16, comps=1, dv=64, qrow=lambda oh, cp: oh, khalf=lambda oh, cp: oh % 2,
                kload=lambda oh: [(0, 256, GK[0, 0, (oh // 2) * 128:(oh // 2 + 1) * 128, 768:1024]),
                                  (256, TOK, KT[(oh // 2) * 128:(oh // 2 + 1) * 128, :]),
                                  (256 + TOK, 768, GK[0, 1, (oh // 2) * 128:(oh // 2 + 1) * 128, 0:768])],
                vload=lambda oh: [(0, 2, vview(GV[0, 0, 768:1024, :], oh * 64, 64)),
                                  (2, TOK // 128, vview(V, oh * 64, 64)),
                                  (2 + TOK // 128, 6, vview(GV[0, 1, 0:768, :], oh * 64, 64))],
                band=na_band, cls=na_cls, tab=dr["natab"])


def emit_A(kb, c, dr, layer, h_src):
    kind = MIX[layer]
    nq, nk, nv = mixer_dims(kind)
    ffn_phase(kb, c, h_src, dr["h"], dr["wg1"], dr["wu1"], dr["wd1"], layer, 0, 1)
    w = dr["w_in"]
    if kind == "na":
        fm = [dict(w=w[:, 0:nq], wsw=None, out=dr["QT"], ncols=nq, mode="plain", scale=0.125),
              dict(w=w[:, nq:nq + nk], wsw=None, out=dr["KT"], ncols=nk, mode="plain", scale=1.0)]
        tabs, gcols = None, None
    else:
        mode = "rope" if kind == "diff" else "normrope"
        wsw = dr["w_sw"]
        fm = [dict(w=w[:, 0:nq], wsw=wsw[:, 0:nq], out=dr["QT"], ncols=nq, mode=mode, tab=(0, 1), g=(0, 1)),
              dict(w=w[:, nq:nq + nk], wsw=wsw[:, nq:nq + nk], out=dr["KT"], ncols=nk, mode=mode, tab=(2, 3), g=(2, 3))]
        tabs = dr["tabs"]
        gcols = dr["gcols"] if kind == "gqa" else None
    qkv_phase(kb, c, dr["h"], layer, fm, dict(w=w[:, nq + nk:nq + nk + nv], out=dr["V"], nv=nv), tabs, gcols)


def emit_X(kb, c, dr, layer):
    kind = MIX[layer]
    nq, nk, nv = mixer_dims(kind)
    if kind == "na":
        KT, V, KH, VH = dr["KT"], dr["V"], dr["KH"], dr["VH"]
        kb.dma("sp", out=KH[:, 0:768], in_=KT[:, 0:768])
        kb.dma("pool", out=KH[:, 768:1024], in_=KT[:, TOK - 256:TOK])
        kb.dma("sp", out=VH[0:768, :], in_=V[0:768, :])
        kb.dma("pool", out=VH[768:1024, :], in_=V[TOK - 256:TOK, :])
        kb.barrier()
        kb.allgather_pairs(KH, dr["GK"][0].rearrange("r p t -> (r p) t"))
        kb.allgather_pairs(VH, dr["GV"][0].rearrange("r p t -> (r p) t"))
    else:
        tpp = v_tpp(nv)
        for i in range(nk // KP_ROWS):
            kb.allgather_pairs(dr["KT"][i * KP_ROWS:(i + 1) * KP_ROWS, :], dr["GK"][i].rearrange("r p t -> (r p) t"))
        for i in range(TOK // tpp):
            kb.allgather_pairs(dr["V"][i * tpp:(i + 1) * tpp, :], dr["GV"][i].rearrange("r p t -> (r p) t"))
    kb.barrier()


def emit_B(kb, c, dr, layer, h_src):
    kind = MIX[layer]
    spec = attn_spec(kind, layer, dr)
    skeys = NA_KEYS if kind == "na" else SEQ
    attn_phase(kb, c, dr["QT"], dr["AT"], spec, skeys=skeys)
    if "O" in os.environ.get("KSKIP", ""):
        return
    outproj_phase(kb, c, h_src, dr["h"], dr["AT"], dr["wo"], layer)
    ffn_phase(kb, c, dr["h"], dr["h"], dr["wg2"], dr["wu2"], dr["wd2"], layer, 4, 5)


def declare(nc, name, shape, dt, kind):
    return nc.dram_tensor(name, list(shape), dt, kind=kind).ap()


def build_fused(layers=(0, 1, 2, 3)):
    nc = bass.Bass("TRN2", target_bir_lowering=False)
    ins = []

    def d_in(name, shape, dt=F32):
        ins.append(name)
        return declare(nc, name, shape, dt, "ExternalInput")

    def d_int(name, shape, dt=BF16):
        return declare(nc, name, shape, dt, "Internal")

    h_in = d_in("h_in", [D, TOK])
    ng = d_in("ng", [128, 4, 6, NKC])
    h = declare(nc, "h", [D, TOK], F32, "ExternalOutput")
    drs = []
    for layer in layers:
        kind = MIX[layer]
        nq, nk, nv = mixer_dims(kind)
        dr = {"h": h}
        sfx = "_L%d" % layer
        dr["wg1"] = d_in("wg1" + sfx, [D, DFF]); dr["wu1"] = d_in("wu1" + sfx, [D, DFF]); dr["wd1"] = d_in("wd1" + sfx, [DFF, D])
        dr["w_in"] = d_in("w_in" + sfx, [D, nq + nk + nv])
        if kind != "na":
            dr["w_sw"] = d_in("w_sw" + sfx, [D, nq + nk])
            dr["tabs"] = d_in("tabs" + sfx, [4, 128, TOK])
        if kind == "gqa":
            dr["gcols"] = d_in("gcols" + sfx, [128, 4])
        dr["wo"] = d_in("wo" + sfx, [D, D])
        dr["wg2"] = d_in("wg2" + sfx, [D, DFF]); dr["wu2"] = d_in("wu2" + sfx, [D, DFF]); dr["wd2"] = d_in("wd2" + sfx, [DFF, D])
        if kind == "diff":
            dr["lam"] = d_in("lam" + sfx, [128, 256]); dr["subg"] = d_in("subg" + sfx, [128, 1])
        if kind == "na":
            dr["natab"] = d_in("natab" + sfx, [3, 16, 128, 8, 512])
        dr["QT"] = d_int("QT" + sfx, [nq, TOK])
        dr["KT"] = d_int("KT" + sfx, [nk, TOK])
        dr["V"] = d_int("V" + sfx, [TOK, nv])
        dr["AT"] = d_int("AT" + sfx, [D, TOK])
        if kind == "na":
            dr["KH"] = d_int("KH" + sfx, [nk, 1024])
            dr["VH"] = d_int("VH" + sfx, [1024, nv])
            dr["GK"] = d_int("GK" + sfx, [1, 2, nk, 1024])
            dr["GV"] = d_int("GV" + sfx, [1, 2, 1024, nv])
        else:
            tpp = v_tpp(nv)
            dr["GK"] = d_int("GK" + sfx, [nk // KP_ROWS, 2, KP_ROWS, TOK])
            dr["GV"] = d_int("GV" + sfx, [TOK // tpp, 2, tpp, nv])
        drs.append(dr)
    with ExitStack() as es:
        kb = KB(nc, es)
        c = setup_consts(kb, es, ng)
        src = h_in
        skip = os.environ.get("KSKIP", "")
        for layer, dr in zip(layers, drs):
            if "A" not in skip:
                emit_A(kb, c, dr, layer, src)
            if "X" not in skip:
                emit_X(kb, c, dr, layer)
            if "B" not in skip:
                emit_B(kb, c, dr, layer, h)
            src = h
        kb.barrier()
    return nc, ins


_PROG_CACHE = {}


def _sw(w_in, kind, nqk):
    perm64 = _partner_perm(kind)
    cols = np.arange(nqk)
    perm = (cols // 64) * 64 + perm64[cols % 64]
    return np.ascontiguousarray(w_in[:, perm])


def kernel(x, norm_g, ffn1_wg, ffn1_wu, ffn1_wd, ffn2_wg, ffn2_wu, ffn2_wd,
           diff_w_in, diff_w_out, diff_lambda, diff_subln,
           na_w_in, na_w_out, na_rpb,
           gqa_w_in, gqa_w_out, gqa_qk_norm):
    f = lambda a: np.ascontiguousarray(np.asarray(a, dtype=np.float32))
    x = f(x); norm_g = f(norm_g)
    W = dict(ffn1_wg=f(ffn1_wg), ffn1_wu=f(ffn1_wu), ffn1_wd=f(ffn1_wd), ffn2_wg=f(ffn2_wg), ffn2_wu=f(ffn2_wu), ffn2_wd=f(ffn2_wd))
    w_in = {"diff": f(diff_w_in), "na": f(na_w_in), "gqa": f(gqa_w_in)}
    w_out = {"diff": f(diff_w_out), "na": f(na_w_out), "gqa": f(gqa_w_out)}
    diff_lambda = f(diff_lambda); diff_subln = f(diff_subln); na_rpb = f(na_rpb); gqa_qk_norm = f(gqa_qk_norm)
    ngl = np.ascontiguousarray(norm_g.reshape(4, 6, NKC, 128).transpose(3, 0, 1, 2))
    cores = list(range(NCORES))
    if "fused" not in _PROG_CACHE:
        _PROG_CACHE["fused"] = build_fused()
    nc, ins = _PROG_CACHE["fused"]

    shared = {"ng": ngl}
    per_half = [dict(), dict()]
    for layer in range(4):
        kind, j = MIX[layer], MIXJ[layer]
        sfx = "_L%d" % layer
        nq, nk, nv = mixer_dims(kind)
        shared["wg1" + sfx] = W["ffn1_wg"][layer]; shared["wu1" + sfx] = W["ffn1_wu"][layer]; shared["wd1" + sfx] = W["ffn1_wd"][layer]
        shared["wg2" + sfx] = W["ffn2_wg"][layer]; shared["wu2" + sfx] = W["ffn2_wu"][layer]; shared["wd2" + sfx] = W["ffn2_wd"][layer]
        shared["w_in" + sfx] = w_in[kind][j]
        shared["wo" + sfx] = w_out[kind][j]
        if kind != "na":
            shared["w_sw" + sfx] = _sw(w_in[kind][j], kind, nq + nk)
            for half in range(2):
                per_half[half]["tabs" + sfx] = rope_tables(kind, half)
        if kind == "gqa":
            p64 = _partner_perm("gqa")
            gq, gk = gqa_qk_norm[j, 0], gqa_qk_norm[j, 1]
            shared["gcols" + sfx] = np.ascontiguousarray(np.stack([np.tile(gq, 2), np.tile(gq[p64], 2), np.tile(gk, 2), np.tile(gk[p64], 2)], 1))
        if kind == "diff":
            shared["lam" + sfx] = np.ascontiguousarray(np.broadcast_to(diff_lambda[j].reshape(1, 256), (128, 256)))
            shared["subg" + sfx] = np.ascontiguousarray(diff_subln[j].reshape(128, 1))
        if kind == "na":
            for half in range(2):
                per_half[half]["natab" + sfx] = na_tables(na_rpb[j], half)
    in_maps = []
    for cc in cores:
        im = dict(shared)
        im.update(per_half[cc % 2])
        im["h_in"] = np.ascontiguousarray(x[cc // 2, (cc % 2) * TOK:(cc % 2 + 1) * TOK, :].T)
        in_maps.append({k: im[k] for k in ins})
    res = run_bass_kernel_spmd(nc, in_maps, core_ids=cores)
    out = np.empty((BATCH, SEQ, D), np.float32)
    for cc in cores:
        out[cc // 2, (cc % 2) * TOK:(cc % 2 + 1) * TOK, :] = np.asarray(res.results[cc]["h"]).T
    return out
```
